# Optimizing a Trainium2 kernel written in Bass

```python
import math
import jax, jax.numpy as jnp
from jax import lax
import numpy as np

D_MODEL = 2048
BATCH = 1
SEQ = 16384
DEPTH = 1

MIX_WIDTH = D_MODEL
A_HEAD_DIM = 128
A_WIDTH = MIX_WIDTH // 2
A_HEADS = A_WIDTH // A_HEAD_DIM
A_PATTERNS = ((128, 1), (512, 4), (2048, 16))
B_HEAD_DIM = 64
B_WIDTH = MIX_WIDTH - A_WIDTH
B_HEADS = B_WIDTH // B_HEAD_DIM
B_KV_HEADS = 2
B_GROUP = B_HEADS // B_KV_HEADS
B_KV_WIDTH = B_KV_HEADS * B_HEAD_DIM
B_WINDOW = 128
IN_WIDTH = 3 * A_WIDTH + B_WIDTH + 2 * B_KV_WIDTH
FF_MULTIPLE = 256
D_FF = ((8 * D_MODEL + 3 * FF_MULTIPLE - 1) // (3 * FF_MULTIPLE)) * FF_MULTIPLE
BLOCK = 128
ROPE_THETA = 10000.0
ALPHA = (2 * DEPTH) ** 0.25
BETA = (8 * DEPTH) ** -0.25
LN_EPS = 1e-5
RMS_EPS = 1e-6

kernel_name = "hymba_dilated_sinkswa_deepnorm_swiglu"


def layer_norm(x, g, b):
    xf = x.astype(jnp.float32)
    mu = xf.mean(-1, keepdims=True)
    var = jnp.square(xf - mu).mean(-1, keepdims=True)
    return ((xf - mu) * lax.rsqrt(var + LN_EPS) * g + b).astype(x.dtype)


def rms_norm(x, g):
    xf = x.astype(jnp.float32)
    return xf * lax.rsqrt(jnp.square(xf).mean(-1, keepdims=True) + RMS_EPS) * g


def rope(x, pos):
    half = x.shape[-1] // 2
    inv_freq = ROPE_THETA ** (-jnp.arange(half, dtype=jnp.float32) / half)
    ang = pos.astype(jnp.float32)[:, None] * inv_freq[None, :]
    cos = jnp.cos(ang)[None, :, None, :]
    sin = jnp.sin(ang)[None, :, None, :]
    x1 = x[..., :half].astype(jnp.float32)
    x2 = x[..., half:].astype(jnp.float32)
    return jnp.concatenate([x1 * cos - x2 * sin, x2 * cos + x1 * sin], axis=-1).astype(x.dtype)


def banded_attention(q, k, v, max_dist):
    N, L, KV, G, D = q.shape
    nb = L // BLOCK
    qb = (q * (1.0 / math.sqrt(D))).reshape(N, nb, BLOCK, KV, G, D)

    def with_prev(t):
        t = t.reshape(N, nb, BLOCK, KV, D)
        prev = jnp.pad(t, ((0, 0), (1, 0), (0, 0), (0, 0), (0, 0)))[:, :-1]
        return jnp.concatenate([prev, t], axis=2)

    kk = with_prev(k)
    vv = with_prev(v)
    s = jnp.einsum('nbqkgd,nbskd->nbkgqs', qb, kk, preferred_element_type=jnp.float32)
    qi = jnp.arange(BLOCK)[:, None]
    kj = jnp.arange(2 * BLOCK)[None, :]
    dist = qi + BLOCK - kj
    band = (dist >= 0) & (dist <= max_dist)
    key_pos = jnp.arange(nb)[:, None] * BLOCK + jnp.arange(2 * BLOCK)[None, :] - BLOCK
    valid = band[None, :, :] & (key_pos >= 0)[:, None, :]
    s = jnp.where(valid[None, :, None, None], s, -jnp.inf)
    mx = s.max(-1)
    p = jnp.exp(s - mx[..., None])
    l = p.sum(-1)
    o = jnp.einsum('nbkgqs,nbskd->nbqkgd', p, vv.astype(jnp.float32))
    mx = jnp.moveaxis(mx, -1, 2)
    l = jnp.moveaxis(l, -1, 2)
    o = o / l[..., None]
    return (o.reshape(N, L, KV, G, D), mx.reshape(N, L, KV, G), l.reshape(N, L, KV, G))


def dilated_mixture_attention(q, k, v):
    B, S, H, D = q.shape
    outs, maxes, dens = [], [], []
    for window, dil in A_PATTERNS:
        L = S // dil
        Lp = -(-L // BLOCK) * BLOCK

        def stride_gather(t):
            t = jnp.swapaxes(t.reshape(B, L, dil, H, D), 1, 2).reshape(B * dil, L, H, D)
            return jnp.pad(t, ((0, 0), (0, Lp - L), (0, 0), (0, 0)))

        def unstride(t):
            t = t[:, :L, :, 0]
            t = t.reshape((B, dil, L) + t.shape[2:])
            return jnp.swapaxes(t, 1, 2).reshape((B, S) + t.shape[3:])

        o, mx, l = banded_attention(stride_gather(q)[:, :, :, None], stride_gather(k),
                                    stride_gather(v), window // dil)
        outs.append(unstride(o))
        maxes.append(unstride(mx))
        dens.append(unstride(l))
    o = jnp.stack(outs)
    mx = jnp.stack(maxes)
    l = jnp.stack(dens)
    w = l * jnp.exp(mx - mx.max(0))
    return (w[..., None] * o).sum(0) / w.sum(0)[..., None]


def sink_window_attention(q, k, v, sinks):
    o, mx, l = banded_attention(q, k, v, B_WINDOW - 1)
    sink = sinks.astype(jnp.float32).reshape(B_KV_HEADS, B_GROUP)
    m_all = jnp.maximum(mx, sink)
    num = l * jnp.exp(mx - m_all)
    return o * (num / (num + jnp.exp(sink - m_all)))[..., None]


def token_mixer(x, w_in, b_in, sinks, g_mix_a, g_mix_b, w_out):
    B, S, _ = x.shape
    pos = jnp.arange(S)
    proj = jnp.einsum('bsd,de->bse', x, w_in) + b_in
    cuts = [A_WIDTH, 2 * A_WIDTH, 3 * A_WIDTH, 3 * A_WIDTH + B_WIDTH, 3 * A_WIDTH + B_WIDTH + B_KV_WIDTH]
    qa, ka, va, qb, kb, vb = jnp.split(proj, cuts, axis=-1)
    qa = rope(qa.reshape(B, S, A_HEADS, A_HEAD_DIM), pos)
    ka = rope(ka.reshape(B, S, A_HEADS, A_HEAD_DIM), pos)
    va = va.reshape(B, S, A_HEADS, A_HEAD_DIM)
    ya = dilated_mixture_attention(qa, ka, va).reshape(B, S, A_WIDTH)
    qb = rope(qb.reshape(B, S, B_HEADS, B_HEAD_DIM), pos).reshape(B, S, B_KV_HEADS, B_GROUP, B_HEAD_DIM)
    kb = rope(kb.reshape(B, S, B_KV_HEADS, B_HEAD_DIM), pos)
    vb = vb.reshape(B, S, B_KV_HEADS, B_HEAD_DIM)
    yb = sink_window_attention(qb, kb, vb, sinks).reshape(B, S, B_WIDTH)
    y = jnp.concatenate([rms_norm(ya, g_mix_a), rms_norm(yb, g_mix_b)], axis=-1).astype(x.dtype)
    return jnp.einsum('bse,ed->bsd', y, w_out)


def swiglu(x, w_up, w_down):
    gate, up = jnp.split(jnp.einsum('bsd,df->bsf', x, w_up), 2, axis=-1)
    return jnp.einsum('bsf,fd->bsd', jax.nn.silu(gate) * up, w_down)


def setup_inputs(seed: int = 0) -> dict:
    key = jax.random.key(seed)
    ks = jax.random.split(key, 14)
    f32 = jnp.float32
    x = jax.random.normal(ks[0], (BATCH, SEQ, D_MODEL), f32)
    col_scale = jnp.asarray(np.concatenate([
        np.ones(2 * A_WIDTH, np.float32), np.full(A_WIDTH, BETA, np.float32),
        np.ones(B_WIDTH + B_KV_WIDTH, np.float32), np.full(B_KV_WIDTH, BETA, np.float32)]))
    w_in = jax.random.normal(ks[1], (DEPTH, D_MODEL, IN_WIDTH), f32) * (D_MODEL ** -0.5) * col_scale
    b_in = 0.02 * jax.random.normal(ks[2], (DEPTH, IN_WIDTH), f32)
    sinks = 0.5 * jax.random.normal(ks[3], (DEPTH, B_HEADS), f32)
    g_mix_a = 1.0 + 0.02 * jax.random.normal(ks[4], (DEPTH, A_WIDTH), f32)
    g_mix_b = 1.0 + 0.02 * jax.random.normal(ks[5], (DEPTH, B_WIDTH), f32)
    w_out = jax.random.normal(ks[6], (DEPTH, MIX_WIDTH, D_MODEL), f32) * (MIX_WIDTH ** -0.5) * BETA
    ln1_g = 1.0 + 0.02 * jax.random.normal(ks[7], (DEPTH, D_MODEL), f32)
    ln1_b = 0.02 * jax.random.normal(ks[8], (DEPTH, D_MODEL), f32)
    w_up = jax.random.normal(ks[9], (DEPTH, D_MODEL, 2 * D_FF), f32) * (D_MODEL ** -0.5) * BETA
    w_down = jax.random.normal(ks[10], (DEPTH, D_FF, D_MODEL), f32) * (D_FF ** -0.5) * BETA
    ln2_g = 1.0 + 0.02 * jax.random.normal(ks[11], (DEPTH, D_MODEL), f32)
    ln2_b = 0.02 * jax.random.normal(ks[12], (DEPTH, D_MODEL), f32)
    return {"x": x, "w_in": w_in, "b_in": b_in, "sinks": sinks, "g_mix_a": g_mix_a,
            "g_mix_b": g_mix_b, "w_out": w_out, "ln1_g": ln1_g, "ln1_b": ln1_b,
            "w_up": w_up, "w_down": w_down, "ln2_g": ln2_g, "ln2_b": ln2_b}


def reference(x, w_in, b_in, sinks, g_mix_a, g_mix_b, w_out, ln1_g, ln1_b, w_up, w_down, ln2_g, ln2_b):
    for i in range(DEPTH):
        mix = token_mixer(x, w_in[i], b_in[i], sinks[i], g_mix_a[i], g_mix_b[i], w_out[i])
        x = layer_norm(ALPHA * x + mix, ln1_g[i], ln1_b[i])
        x = layer_norm(ALPHA * x + swiglu(x, w_up[i], w_down[i]), ln2_g[i], ln2_b[i])
    return x
```

```python
import math
from contextlib import ExitStack

import numpy as np
import concourse.bass as bass
import concourse.mybir as mybir
from concourse.bass_utils import run_bass_kernel_spmd

F32 = mybir.dt.float32
BF16 = mybir.dt.bfloat16
AF = mybir.ActivationFunctionType
ALU = mybir.AluOpType

NCORES = 8
S = 16384
D = 2048
TOK = S // NCORES
KC = D // 128
DFF = 5632
FC = DFF // 128
NCH_IN = 35
ALPHA = 2.0 ** 0.25
LN_EPS = 1e-5
RMS_EPS = 1e-6
NEG = -30000.0
NMB = 16


class Buf:
    __slots__ = ("name", "w", "r", "rd")

    def __init__(self, name):
        self.name = name
        self.w = None
        self.r = {}
        self.rd = []


class Op:
    __slots__ = ("eng", "fn", "deps", "is_dma", "semkey", "val", "needs_inc")


class Sched:
    def __init__(self, nc, stack):
        self.nc = nc
        self.stack = stack
        self.ops = []
        self.dma_cnt = {}
        self.bar_deps = set()
        self.bar_done = set()
        self.dma_since_bar = []
        self.last = {}

    def _add(self, eng, fn, reads, writes, semkey=None):
        idx = len(self.ops)
        deps = set()
        for b in reads:
            if b.w is not None:
                deps.add(b.w)
        for b in writes:
            if b.w is not None:
                deps.add(b.w)
            deps.update(b.r.values())
            deps.update(b.rd)
        if eng not in self.bar_done:
            deps |= self.bar_deps
            self.bar_done.add(eng)
        op = Op()
        op.eng = eng
        op.fn = fn
        op.deps = deps
        op.is_dma = semkey is not None
        op.semkey = semkey
        op.val = 0
        op.needs_inc = False
        if op.is_dma:
            c = self.dma_cnt.get(semkey, 0) + 16
            self.dma_cnt[semkey] = c
            op.val = c
            self.dma_since_bar.append(idx)
        self.ops.append(op)
        for b in reads:
            if op.is_dma:
                b.rd.append(idx)
            else:
                b.r[eng] = idx
        for b in writes:
            b.w = idx
            b.r = {}
            b.rd = []
        self.last[eng] = idx
        return idx

    def pe(self, fn, reads, writes):
        return self._add("pe", fn, reads, writes)

    def act(self, fn, reads, writes):
        return self._add("act", fn, reads, writes)

    def dve(self, fn, reads, writes):
        return self._add("dve", fn, reads, writes)

    def dma(self, q, fn, reads, writes, semkey):
        return self._add(q, fn, reads, writes, semkey=semkey)

    def barrier(self):
        deps = set(self.dma_since_bar)
        for e, i in self.last.items():
            deps.add(i)
        self.bar_deps = self.bar_deps | deps
        self.bar_done = set()
        self.dma_since_bar = []

    def emit(self, final_wait_ops):
        nc = self.nc
        engs = {"pe": nc.tensor, "act": nc.scalar, "dve": nc.vector, "pool": nc.gpsimd, "sp": nc.sync}
        for op in self.ops:
            for d in op.deps:
                dop = self.ops[d]
                if not dop.is_dma:
                    if dop.eng == op.eng and op.eng == "pe":
                        continue
                    dop.needs_inc = True
        cnt = {e: 0 for e in engs}
        for op in self.ops:
            if (not op.is_dma) and op.needs_inc:
                cnt[op.eng] += 1
                op.val = cnt[op.eng]
        esem = {e: self.stack.enter_context(nc.semaphore("e_" + e)) for e in ("pe", "act", "dve")}
        dsem = {}
        for k in self.dma_cnt:
            dsem[k] = self.stack.enter_context(nc.semaphore("d_" + str(k)))
        waited = {e: {} for e in engs}
        for op in self.ops:
            stream = engs[op.eng]
            need = {}
            for d in op.deps:
                dop = self.ops[d]
                if dop.is_dma:
                    key = ("d", dop.semkey)
                else:
                    if dop.eng == op.eng and op.eng == "pe":
                        continue
                    key = ("e", dop.eng)
                if dop.val > need.get(key, 0):
                    need[key] = dop.val
            w = waited[op.eng]
            for key, val in need.items():
                if w.get(key, 0) < val:
                    sem = dsem[key[1]] if key[0] == "d" else esem[key[1]]
                    stream.wait_ge(sem, val)
                    w[key] = val
            ins = op.fn()
            if op.is_dma:
                ins.then_inc(dsem[op.semkey], 16)
            elif op.needs_inc:
                ins.then_inc(esem[op.eng], 1)
        for i in final_wait_ops:
            op = self.ops[i]
            nc.sync.wait_ge(dsem[op.semkey], op.val)


def build(debug=False):
    nc = bass.Bass("TRN2", target_bir_lowering=False)
    dt = nc.dram_tensor
    xT_d = dt("xT", [2, 128, KC, TOK], F32, kind="ExternalInput").ap()
    xres_d = dt("xres", [KC, 128, TOK], F32, kind="ExternalInput").ap()
    ropeA_d = dt("ropeA", [2, 128, 2, TOK], F32, kind="ExternalInput").ap()
    ropeB_d = dt("ropeB", [2, 128, 2, TOK], F32, kind="ExternalInput").ap()
    maskb_d = dt("maskb", [128, NMB, 512], F32, kind="ExternalInput").ap()
    consts_d = dt("consts", [128, 6, 128], F32, kind="ExternalInput").ap()
    win_d = dt("win", [NCH_IN, 128, KC, 128], F32, kind="ExternalInput").ap()
    bin_d = dt("bin", [128, NCH_IN], F32, kind="ExternalInput").ap()
    wout_d = dt("wout", [KC, 128, KC, 128], F32, kind="ExternalInput").ap()
    wup_d = dt("wup", [FC, 128, KC, 256], F32, kind="ExternalInput").ap()
    wdn_d = dt("wdn", [KC, 128, FC, 128], F32, kind="ExternalInput").ap()
    gmix_d = dt("gmix", [128, KC], F32, kind="ExternalInput").ap()
    lnp_d = dt("lnp", [128, 4, KC], F32, kind="ExternalInput").ap()
    sinks_d = dt("sinks", [128, 8], F32, kind="ExternalInput").ap()
    outT_d = dt("outT", [KC, 128, TOK], F32, kind="ExternalOutput").ap()
    kvs_d = dt("kvs", [19, 128, 2 * TOK], BF16, kind="Internal").ap()
    ys_d = dt("ys", [KC, 128, TOK], BF16, kind=("ExternalOutput" if debug else "Internal")).ap()
    x1s_d = dt("x1s", [KC, 128, TOK], F32, kind="Internal").ap()

    stack = ExitStack()
    with stack:
        S_ = Sched(nc, stack)
        ARENA_EL = 106400
        arena = stack.enter_context(nc.sbuf_tensor("arena", [128, ARENA_EL], BF16))
        apos = {"o": 0}

        def sb(name, shape, dty):
            n = 1
            for s_ in shape[1:]:
                n *= s_
            nel = n * (2 if dty == F32 else 1)
            nel = (nel + 15) // 16 * 16
            o = apos["o"]
            assert o + nel <= ARENA_EL, (name, o, nel)
            apos["o"] = o + nel
            v = arena[:, o:o + nel]
            if dty == F32:
                v = v.bitcast(F32)
            v = v[:, 0:n]
            if len(shape) == 3:
                v = v.rearrange("p (a b) -> p a b", a=shape[1])
            return v
        PB = [stack.enter_context(nc.psum_tensor("pb%d" % i, [128, 512], F32)) for i in range(7)]
        PT = stack.enter_context(nc.psum_tensor("pt", [128, 1024], BF16))
        bPB = [Buf("pb%d" % i) for i in range(7)]
        bPT = Buf("pt")

        cst = sb("cst", [128, 6, 128], BF16)
        bcst = Buf("cst")
        IDENT, PERMA, PERMB, ONES, ONESL, ONESR = (cst[:, i, :] for i in range(6))
        binb = sb("binb", [128, NCH_IN], F32)
        gmix = sb("gmix", [128, KC], F32)
        lnp = sb("lnp", [128, 4, KC], F32)
        esink = sb("esink", [128, 8], F32)
        bsmall = Buf("small")
        S_.dma("pool", lambda: nc.gpsimd.dma_start(out=cst[:], in_=consts_d[:, :, :]), [], [bcst], "cst")
        S_.dma("sp", lambda: nc.sync.dma_start(out=binb[:], in_=bin_d[:, :]), [], [bsmall], "small")
        S_.dma("sp", lambda: nc.sync.dma_start(out=gmix[:], in_=gmix_d[:, :]), [], [bsmall], "small")
        S_.dma("sp", lambda: nc.sync.dma_start(out=lnp[:], in_=lnp_d[:, :, :]), [], [bsmall], "small")
        S_.dma("sp", lambda: nc.sync.dma_start(out=esink[:], in_=sinks_d[:, :]), [], [bsmall], "small")
        S_.act(lambda: nc.scalar.activation(out=esink[:], in_=esink[:], func=AF.Exp), [bsmall], [bsmall])
        S_.dve(lambda: nc.vector.tensor_scalar(out=gmix[:], in0=gmix[:], scalar1=32.0, scalar2=None,
                                               op0=ALU.mult), [bsmall], [bsmall])

        base_persist = apos["o"]
        if True:
            sb1 = sb
            xT = sb1("xT", [128, KC, TOK], BF16)
            bxT = Buf("xT")
            ropeA = sb1("ropeA", [128, 2, TOK], F32)
            ropeB = sb1("ropeB", [128, 2, TOK], F32)
            brope = Buf("rope")
            maskb = sb1("maskb", [128, NMB, 512], BF16)
            bmask = Buf("maskb")
            NW = 3
            wbuf = [sb1("w%d" % i, [128, KC, 128], BF16) for i in range(NW)]
            bw = [Buf("w%d" % i) for i in range(NW)]
            pre = [sb1("pre%d" % i, [128, 512], BF16) for i in range(2)]
            bpre = [Buf("pre%d" % i) for i in range(2)]
            t1 = [sb1("t1_%d" % i, [128, 512], F32) for i in range(2)]
            t2 = [sb1("t2_%d" % i, [128, 512], F32) for i in range(1)]
            bt1 = [Buf("t1_%d" % i) for i in range(2)]
            bt2 = [Buf("t2_%d" % i) for i in range(1)]
            kvout = [sb1("kvo%d" % i, [128, TOK], BF16) for i in range(2)]
            bkvout = [Buf("kvo%d" % i) for i in range(2)]
            o_halo = apos["o"]
            xTh = sb1("xTh", [128, KC, TOK], BF16)
            bxTh = Buf("xTh")


            state = {"w": 0, "pre": 0, "acc": 0}

            def load_w(ch):
                i = state["w"] % NW
                state["w"] += 1
                S_.dma("pool", lambda i=i, ch=ch: nc.gpsimd.dma_start(out=wbuf[i][:], in_=win_d[ch]),
                       [], [bw[i]], "w%d" % i)
                return i

            def proj_tile(wi, ch, tt, rope_tab, perm, dst_ap, bdst, xsrc=None, bxsrc=None):
                a = state["acc"] % 2
                state["acc"] += 1
                pb = PB[a]
                if xsrc is None:
                    xsrc, bxsrc = xT, bxT
                if isinstance(bxsrc, list):
                    bxsrc = bxsrc[tt]

                def mm(wi=wi, tt=tt, pb=pb, xsrc=xsrc):
                    ins = None
                    for kc in range(KC):
                        ins = nc.tensor.matmul(pb[:], wbuf[wi][:, kc, :], xsrc[:, kc, tt * 512:(tt + 1) * 512],
                                               start=(kc == 0), stop=(kc == KC - 1))
                    return ins
                S_.pe(mm, [bw[wi]] + (bxsrc if isinstance(bxsrc, list) else [bxsrc]), [bPB[a]])
                flush_rope()
                if rope_tab is None:
                    S_.act(lambda pb=pb, ch=ch: nc.scalar.activation(out=dst_ap, in_=pb[:], func=AF.Identity,
                                                                     bias=binb[:, ch:ch + 1], scale=1.0),
                           [bPB[a], bsmall], [bdst])
                    return
                pi = state["pre"] % 2
                state["pre"] += 1
                S_.act(lambda pb=pb, ch=ch, pi=pi: nc.scalar.activation(out=pre[pi][:], in_=pb[:], func=AF.Identity,
                                                                        bias=binb[:, ch:ch + 1], scale=1.0),
                       [bPB[a], bsmall], [bpre[pi]])
                cs = rope_tab[:, 0, tt * 512:(tt + 1) * 512]
                sn = rope_tab[:, 1, tt * 512:(tt + 1) * 512]

                def stage_b(pi=pi, cs=cs, sn=sn, dst_ap=dst_ap, bdst=bdst, perm=perm):
                    S_.pe(lambda: nc.tensor.matmul(PB[2][:], perm, pre[pi][:], start=True, stop=True),
                          [bpre[pi], bcst], [bPB[2]])
                    S_.dve(lambda: nc.vector.tensor_tensor(out=t1[pi][:], in0=pre[pi][:], in1=cs, op=ALU.mult),
                           [bpre[pi], brope], [bt1[pi]])
                    S_.dve(lambda: nc.vector.tensor_tensor(out=t2[0][:], in0=PB[2][:], in1=sn, op=ALU.mult),
                           [bPB[2], brope], [bt2[0]])
                    S_.dve(lambda: nc.vector.tensor_tensor(out=dst_ap, in0=t1[pi][:], in1=t2[0][:], op=ALU.add),
                           [bt1[pi], bt2[0]], [bdst])
                state["rope_pending"] = stage_b

            def flush_rope():
                f = state.get("rope_pending")
                if f is not None:
                    state["rope_pending"] = None
                    f()

            kv_chunks = list(range(8, 24)) + [32, 33, 34]
            kv_order = list(range(8, 16)) + [18] + list(range(0, 8)) + [16, 17]
            kvs_store_ops = []
            xT2d = xT.rearrange("p a b -> p (a b)")
            stg = [xT2d[:, i * 16384:(i + 1) * 16384].bitcast(F32).rearrange("p (a b) -> p a b", a=KC) for i in range(2)]
            bstg = [Buf("stg0"), Buf("stg1")]
            bxTh_t = [[Buf("xTh_t%d_a" % t_), Buf("xTh_t%d_b" % t_)] for t_ in range(4)]
            S_.dma("pool", lambda: nc.gpsimd.dma_start(out=xTh[:, :, 1536:2048], in_=xT_d[0, :, :, 1536:2048]),
                   [], [bxTh_t[3][0], bxTh_t[3][1]], "xTh3")
            for t_ in range(3):
                s_ = t_ % 2
                S_.dma("sp", lambda t_=t_, s_=s_: nc.sync.dma_start(out=stg[s_], in_=xT_d[0, :, :, t_ * 512:(t_ + 1) * 512]),
                       [], [bstg[s_]], "stg%d" % s_)
                S_.act(lambda t_=t_, s_=s_: nc.scalar.copy(out=xTh[:, 0:8, t_ * 512:(t_ + 1) * 512], in_=stg[s_][:, 0:8, :]),
                       [bstg[s_]], [bxTh_t[t_][0]])
                S_.dve(lambda t_=t_, s_=s_: nc.vector.tensor_copy(out=xTh[:, 8:16, t_ * 512:(t_ + 1) * 512],
                                                                  in_=stg[s_][:, 8:16, :]), [bstg[s_]], [bxTh_t[t_][1]])
            for half in range(2):
                xs_, bxs_ = (xTh, bxTh_t) if half == 0 else (xT, bxT)
                rdep = [bstg[0]] if half == 0 else []
                S_.dma("sp", lambda half=half: nc.sync.dma_start(out=ropeA[:], in_=ropeA_d[half]), rdep, [brope], "rope")
                S_.dma("sp", lambda half=half: nc.sync.dma_start(out=ropeB[:], in_=ropeB_d[half]), rdep, [brope], "rope")
                nxt = load_w(kv_chunks[kv_order[0]])
                for oi_, ci in enumerate(kv_order):
                    ch = kv_chunks[ci]
                    wi = nxt
                    if oi_ + 1 < len(kv_order):
                        nxt = load_w(kv_chunks[kv_order[oi_ + 1]])
                    o = oi_ % 2
                    if 8 <= ch < 16:
                        tab, perm = ropeA, PERMA
                    elif ch in (32, 33):
                        tab, perm = ropeB, PERMB
                    else:
                        tab, perm = None, None
                    for tt in range(4):
                        if half == 0 and ch >= 32 and tt < 3:
                            continue
                        proj_tile(wi, ch, tt, tab, perm, kvout[o][:, tt * 512:(tt + 1) * 512], bkvout[o], xs_, bxs_)
                    if half == 0 and oi_ == 17:
                        S_.dma("pool", lambda: nc.gpsimd.dma_start(out=maskb[:], in_=maskb_d[:, :, :]), [], [bmask], "maskb")
                    if half == 0 and 1 <= oi_ <= KC:
                        kc = oi_ - 1
                        S_.dma("pool", lambda kc=kc: nc.gpsimd.dma_start(out=xT[:, kc, :], in_=xT_d[1, :, kc, :]),
                               [], [bxT, bstg[kc // 8]], "xT")
                    flush_rope()
                    bkv = Buf("kvs_%d_%d" % (ci, half))
                    k = S_.dma("sp", lambda o=o, ci=ci, half=half: nc.sync.dma_start(
                        out=kvs_d[ci, :, half * TOK:(half + 1) * TOK], in_=kvout[o][:]),
                        [bkvout[o]], [bkv], "kvo%d" % o)
                    kvs_store_ops.append(k)
            S_.barrier()

            apos["o"] = o_halo
            if True:
                sb2 = sb
                qT = sb2("qT", [128, TOK], BF16)
                kT = sb2("kT", [128, 2 * TOK], BF16)
                vT = sb2("vT", [128, 2 * TOK], BF16)
                bq, bk, bv = Buf("qT"), Buf("kT"), Buf("vT")
                vblks = [sb2("vblk%d" % i, [128, 69, 128], BF16) for i in range(2)]
                bvblks = [Buf("vblk0"), Buf("vblk1")]
                pbat = [sb2("pbat%d" % i, [128, 512], BF16) for i in range(3)]
                bpbat = [Buf("pbat%d" % i) for i in range(3)]
                SBANK = [3, 4, 0]
                lsb = sb2("lsb", [128, 512], F32)
                blsb = Buf("lsb")
                yah = sb2("yah", [128, TOK], BF16)
                byah = Buf("yah")
                state["sb"] = 0

                def s_batch(tiles, sel, mask_idx, scale):
                    si = state["sb"] % 3
                    state["sb"] += 1
                    bank = PB[SBANK[si]]

                    def mm(tiles=tiles, bank=bank, mask_idx=mask_idx):
                        first = True
                        ins = None
                        nt = len(tiles)
                        for i, (l, r) in enumerate(tiles):
                            ins = nc.tensor.matmul(sel(bank[:, :], i), l, r, start=first, stop=(i == nt - 1),
                                                   skip_group_check=True)
                            first = False
                        return ins
                    S_.pe(mm, [bq, bk], [bPB[SBANK[si]]])
                    S_.act(lambda bank=bank, si=si: nc.scalar.activation(out=pbat[si][:], in_=bank[:], func=AF.Exp,
                                                                         scale=scale),
                           [bPB[SBANK[si]]], [bpbat[si]])
                    S_.dve(lambda si=si, mask_idx=mask_idx: nc.vector.tensor_tensor(
                        out=pbat[si][:], in0=pbat[si][:], in1=maskb[:, mask_idx, :], op=ALU.mult),
                        [bpbat[si], bmask], [bpbat[si]])
                    return si

                OL = [(5, 6), (1, 2)]

                def pv_batch(si, vtiles, sel, lones, oset, first_in_bank, last_in_bank, bvblk):
                    ob, lb = OL[oset]

                    def mm(si=si, vtiles=vtiles):
                        f = first_in_bank
                        nt = len(vtiles)
                        for i, va in enumerate(vtiles):
                            nc.tensor.matmul(sel(PB[ob][:, :], i), va, sel(pbat[si][:, :], i), start=f,
                                             stop=(last_in_bank and i == nt - 1), skip_group_check=True)
                            f = False
                        return nc.tensor.matmul(PB[lb][:], lones, pbat[si][:], start=first_in_bank,
                                                stop=last_in_bank, skip_group_check=True)
                    S_.pe(mm, [bpbat[si], bvblk, bcst], [bPB[ob], bPB[lb]])

                def run_pipeline(items, evac):
                    pend_ = []
                    for it in items + [None, None]:
                        if it is not None:
                            si = s_batch(it["tl"], it["sel"], it["mi"], it["scale"])
                            pend_.append((it, si))
                        if len(pend_) > 2 or (it is None and pend_):
                            pit, psi = pend_.pop(0)
                            pv_batch(psi, pit["vl"], pit["sel"], pit["lones"], (pit["k"] + 1) % 2, pit["first"], pit["last"], pit["bv"])
                            if pit["qend"]:
                                evac(pit["k"])

                PTs = [PT[:, :], PB[0][:, :].bitcast(BF16)]
                bPTh = [bPT, bPB[0]]
                state["pt"] = 0

                def transposes(src, col_lists, dst_base, vblk, bvblk):
                    i = 0
                    while i < len(col_lists):
                        grp = col_lists[i:i + 8]
                        hf = state["pt"] % 2
                        state["pt"] += 1

                        def mm(grp=grp, hf=hf):
                            ins = None
                            for j, cap in enumerate(grp):
                                ins = nc.tensor.transpose(PTs[hf][:, j * 128:(j + 1) * 128], cap, IDENT)
                            return ins
                        S_.pe(mm, [src[1], bcst], [bPTh[hf]])
                        n = len(grp)
                        S_.dve(lambda n=n, i=i, hf=hf: nc.vector.tensor_copy(
                            out=vblk[:, dst_base + i:dst_base + i + n, :],
                            in_=PTs[hf][:, 0:n * 128].rearrange("p (j d) -> p j d", d=128)), [bPTh[hf]], [bvblk])
                        i += 8

                def strided(t, base, res, step, n):
                    if step == 1:
                        return t[:, base + res:base + res + n]
                    return t[:, base:base + step * n].rearrange("p (m s) -> p s m", s=step)[:, res, :]

                ys_store_ops = []
                for h in range(8):
                    wi = load_w(h)
                    S_.dma("sp", lambda h=h: nc.sync.dma_start(out=kT[:], in_=kvs_d[h]), [], [bk], "kT")
                    S_.dma("sp", lambda h=h: nc.sync.dma_start(out=vT[:], in_=kvs_d[8 + h]), [], [bv], "vT")
                    for tt in range(4):
                        proj_tile(wi, h, tt, ropeA, PERMA, qT[:, tt * 512:(tt + 1) * 512], bq)
                    cols = []
                    for b in range(-1, 16):
                        cols.append(strided(vT, TOK + 128 * b, 0, 1, 128))
                    for c in range(4):
                        for j in range(-1, 4):
                            cols.append(strided(vT, TOK + 512 * j, c, 4, 128))
                    for r in range(16):
                        cols.append(strided(vT, 0, r, 16, 128))
                        cols.append(strided(vT, TOK, r, 16, 128))
                    vblk, bvblk = vblks[h % 2], bvblks[h % 2]
                    transposes((vT, bv), cols, 0, vblk, bvblk)
                    flush_rope()
                    scale = 1.0 / math.sqrt(128.0)
                    items = []
                    selp = lambda t, i: t[:, 128 * i:128 * i + 128]
                    sel2 = lambda t, i: t.rearrange("p (i c) -> p c i", c=4)[:, i, :]
                    sel3 = lambda t, i: t.rearrange("p (a r) -> p r a", r=16)[:, i, :]
                    for k in range(4):
                        batches = []
                        for prev in (True, False):
                            tl, vl = [], []
                            for qb in range(4 * k, 4 * k + 4):
                                kb = qb - 1 if prev else qb
                                tl.append((kT[:, TOK + 128 * kb:TOK + 128 * kb + 128], qT[:, 128 * qb:128 * qb + 128]))
                                vl.append(vblk[:, kb + 1, :])
                            batches.append((tl, vl, ((1 if k == 0 else 0) if prev else 2), selp))
                        for prev in (True, False):
                            tl, vl = [], []
                            for c in range(4):
                                j = k - 1 if prev else k
                                tl.append((strided(kT, TOK + 512 * j, c, 4, 128), strided(qT, 512 * k, c, 4, 128)))
                                vl.append(vblk[:, 17 + c * 5 + (j + 1), :])
                            batches.append((tl, vl, ((4 if k == 0 else 3) if prev else 5), sel2))
                        for prev in (True, False):
                            tl, vl = [], []
                            for r in range(16):
                                kbase = 0 if prev else TOK
                                tl.append((strided(kT, kbase, r, 16, 128), strided(qT, 512 * k, r, 16, 32)))
                                vl.append(vblk[:, 37 + r * 2 + (0 if prev else 1), :])
                            batches.append((tl, vl, ((6 + k) if prev else (10 + k)), sel3))
                        for bi, (tl, vl, mi, sel_) in enumerate(batches):
                            items.append(dict(tl=tl, vl=vl, sel=sel_, mi=mi, lones=ONES, scale=scale, first=(bi == 0),
                                              last=(bi == len(batches) - 1), k=k, qend=(bi == len(batches) - 1), bv=bvblk))

                    def evacA(k):
                        ob, lb = OL[(k + 1) % 2]
                        S_.act(lambda: nc.scalar.activation(out=lsb[:], in_=PB[lb][:], func=AF.Ln), [bPB[lb]], [blsb])
                        S_.act(lambda: nc.scalar.activation(out=lsb[:], in_=lsb[:], func=AF.Exp, scale=-1.0), [blsb], [blsb])
                        S_.dve(lambda k=k: nc.vector.tensor_tensor(out=yah[:, k * 512:(k + 1) * 512], in0=PB[ob][:],
                                                                   in1=lsb[:], op=ALU.mult),
                               [bPB[ob], blsb], [byah])
                    run_pipeline(items, evacA)
                    bys = Buf("ys%d" % h)
                    ys_store_ops.append(S_.dma("sp", lambda h=h: nc.sync.dma_start(out=ys_d[h], in_=yah[:]),
                                               [byah], [bys], "yah"))

                scaleb = 1.0 / math.sqrt(64.0)
                for j in range(8):
                    g = j // 4
                    vblk, bvblk = vblks[g], bvblks[g]
                    wi = load_w(24 + j)
                    if j % 4 == 0:
                        S_.dma("sp", lambda g=g: nc.sync.dma_start(out=kT[:], in_=kvs_d[16 + g]), [], [bk], "kT")
                        if j == 0:
                            S_.dma("sp", lambda: nc.sync.dma_start(out=vT[:], in_=kvs_d[18]), [], [bv], "vT")
                        S_.dve(lambda vblk=vblk: nc.vector.memset(vblk[:, 0:34, :], 0.0), [], [bvblk])
                        for b0 in range(-1, 16, 8):
                            bl = list(range(b0, min(b0 + 8, 16)))
                            hf = state["pt"] % 2
                            state["pt"] += 1

                            def mm(bl=bl, hf=hf):
                                ins = None
                                for jj, b in enumerate(bl):
                                    ins = nc.tensor.transpose(PTs[hf][:, jj * 128:(jj + 1) * 128],
                                                              vT[:, TOK + 128 * b:TOK + 128 * b + 128], IDENT)
                                return ins
                            S_.pe(mm, [bv, bcst], [bPTh[hf]])
                            n = len(bl)
                            ptv = PTs[hf][:, 0:n * 128].rearrange("p (j d) -> p j d", d=128)
                            S_.dve(lambda n=n, b0=b0, ptv=ptv, g=g, vblk=vblk: nc.vector.tensor_copy(
                                out=vblk[:, b0 + 1:b0 + 1 + n, 0:64], in_=ptv[:, :, g * 64:(g + 1) * 64]), [bPTh[hf]], [bvblk])
                            S_.dve(lambda n=n, b0=b0, ptv=ptv, g=g, vblk=vblk: nc.vector.tensor_copy(
                                out=vblk[:, 17 + b0 + 1:17 + b0 + 1 + n, 64:128], in_=ptv[:, :, g * 64:(g + 1) * 64]),
                                [bPTh[hf]], [bvblk])
                    for tt in range(4):
                        proj_tile(wi, 24 + j, tt, ropeB, PERMB, qT[:, tt * 512:(tt + 1) * 512], bq)
                    flush_rope()
                    items = []
                    selp = lambda t, i: t[:, 128 * i:128 * i + 128]
                    for k in range(4):
                        nb = 0
                        for e in range(2):
                            lo, hi = 64 * e, 64 * e + 64
                            for prev in (True, False):
                                tl, vl = [], []
                                for qb in range(4 * k, 4 * k + 4):
                                    kb = qb - 1 if prev else qb
                                    tl.append((kT[lo:hi, TOK + 128 * kb:TOK + 128 * kb + 128],
                                               qT[lo:hi, 128 * qb:128 * qb + 128]))
                                    vl.append(vblk[:, e * 17 + kb + 1, :])
                                mi = (15 if k == 0 else 14) if prev else 2
                                items.append(dict(tl=tl, vl=vl, sel=selp, mi=mi, lones=(ONESL if e == 0 else ONESR),
                                                  scale=scaleb, first=(nb == 0), last=(nb == 3), k=k, qend=(nb == 3), bv=bvblk))
                                nb += 1

                    def evacB(k, j=j):
                        ob, lb = OL[(k + 1) % 2]
                        S_.act(lambda: nc.scalar.activation(out=lsb[:], in_=PB[lb][:], func=AF.Ln, bias=esink[:, j:j + 1],
                                                            scale=1.0), [bPB[lb], bsmall], [blsb])
                        S_.act(lambda: nc.scalar.activation(out=lsb[:], in_=lsb[:], func=AF.Exp, scale=-1.0), [blsb], [blsb])
                        S_.dve(lambda k=k: nc.vector.tensor_tensor(out=yah[:, k * 512:(k + 1) * 512], in0=PB[ob][:],
                                                                   in1=lsb[:], op=ALU.mult),
                               [bPB[ob], blsb], [byah])
                    run_pipeline(items, evacB)
                    bys = Buf("ys%d" % (8 + j))
                    ys_store_ops.append(S_.dma("sp", lambda j=j: nc.sync.dma_start(out=ys_d[8 + j], in_=yah[:]),
                                               [byah], [bys], "yah"))
        S_.barrier()
        apos["o"] = base_persist

        final_ops = list(ys_store_ops)
        if not debug or debug == "full":
            final_ops = phase2(nc, S_, stack, locals())
        S_.emit(final_ops)
    return nc


def phase2(nc, S_, stack, L):
    PB, bPB = L["PB"], L["bPB"]
    ONES = L["ONES"]
    bcst, bsmall = L["bcst"], L["bsmall"]
    gmix, lnp = L["gmix"], L["lnp"]
    ys_d, xres_d, x1s_d, outT_d = L["ys_d"], L["xres_d"], L["x1s_d"], L["outT_d"]
    wout_d, wup_d, wdn_d = L["wout_d"], L["wup_d"], L["wdn_d"]
    TS = 1024
    sb = L["sb"]
    RA2 = sb("RA", [128, KC * 2 * TS], BF16)
    RA = RA2.rearrange("p (a b) -> p a b", a=KC)
    RAf = RA2.bitcast(F32).rearrange("p (a b) -> p a b", a=KC)
    def x1b(n, hh):
        o_ = (n // 2) * 2 * TS + hh * TS + (n % 2) * 512
        return RA2[:, o_:o_ + 512]
    bRAh = [[Buf("RA%d_%d" % (n, h_)) for h_ in range(2)] for n in range(KC)]
    bRA = [b_ for pr in bRAh for b_ in pr]
    RD = sb("RD", [128, FC, TS], BF16)
    bRD = [Buf("RD%d" % f) for f in range(FC)]
    stat = sb("stat", [128, 2, TS], F32)
    bstat = [Buf("stat0"), Buf("stat1")]
    sq = [sb("sq%d" % i, [128, 512], BF16) for i in range(2)]
    zb = [sb("zb%d" % i, [128, 512], BF16) for i in range(2)]
    bsq = [Buf("sq%d" % i) for i in range(2)]
    bzb = [Buf("zb%d" % i) for i in range(2)]
    xr = [sb("xr%d" % i, [128, 512], F32) for i in range(2)]
    bxr = [Buf("xr%d" % i) for i in range(2)]
    tt_ = [sb("tt%d" % i, [128, 512], F32) for i in range(2)]
    btt = [Buf("tt%d" % i) for i in range(2)]
    of = [sb("of%d" % i, [128, 512], F32) for i in range(2)]
    bof = [Buf("of%d" % i) for i in range(2)]
    sg = [sb("sg%d" % i, [128, 512], F32) for i in range(2)]
    bsg = [Buf("sg%d" % i) for i in range(2)]
    WSL = FC * 128
    wsl = [sb("wsl%d" % i, [128, WSL], BF16) for i in range(2)]
    bwsl = [Buf("wsl%d" % i) for i in range(2)]
    ctr = {"sq": 0, "zb": 0, "xr": 0, "tt": 0, "of": 0, "sg": 0, "w": 0, "nt": 0, "lt": 0}
    nt_slots = [(tt_[0][:, :], btt[0]), (tt_[1][:, :], btt[1]), (of[0][:, :], bof[0]), (of[1][:, :], bof[1])]
    out_ops = []
    bx1s = {}

    def nxt(name, n):
        i = ctr[name] % n
        ctr[name] += 1
        return i

    def wslot(kind):
        i = nxt("w", 2)
        if kind == "wo":
            v = wsl[i][:, 0:KC * 128].rearrange("p (a b) -> p a b", a=KC)
        elif kind == "wu":
            v = wsl[i][:, 0:KC * 256].rearrange("p (a b) -> p a b", a=KC)
        else:
            v = wsl[i][:, 0:FC * 128].rearrange("p (a b) -> p a b", a=FC)
        return i, v

    def stats_accum(n, hh, first, last):
        src_ap = RAf[:, n, hh * 512:(hh + 1) * 512]
        zi = nxt("zb", 2)
        qi = nxt("sq", 2)
        S_.act(lambda: nc.scalar.copy(out=zb[zi][:, :], in_=src_ap), [bRAh[n][hh]], [bzb[zi]])
        S_.act(lambda: nc.scalar.activation(out=sq[qi][:, :], in_=src_ap, func=AF.Square), [bRAh[n][hh]], [bsq[qi]])

        def mm():
            nc.tensor.matmul(PB[3 + hh][:], ONES, zb[zi][:, :], start=first, stop=last)
            return nc.tensor.matmul(PB[5 + hh][:], ONES, sq[qi][:, :], start=first, stop=last)
        S_.pe(mm, [bzb[zi], bsq[qi], bcst], [bPB[3 + hh], bPB[5 + hh]])

    def ln_finish():
        for hh in range(2):
            sl = slice(hh * 512, (hh + 1) * 512)
            ti = nxt("tt", 2)
            S_.dve(lambda hh=hh, sl=sl: nc.vector.tensor_scalar(out=stat[:, 0, sl], in0=PB[3 + hh][:], scalar1=1.0 / D,
                                                               scalar2=None, op0=ALU.mult), [bPB[3 + hh]], [bstat[0]])
            S_.dve(lambda sl=sl, ti=ti: nc.vector.tensor_tensor(out=tt_[ti][:, :], in0=stat[:, 0, sl], in1=stat[:, 0, sl],
                                                                op=ALU.mult), [bstat[0]], [btt[ti]])
            S_.dve(lambda hh=hh, sl=sl, ti=ti: nc.vector.scalar_tensor_tensor(
                out=stat[:, 1, sl], in0=PB[5 + hh][:], scalar=1.0 / D, in1=tt_[ti][:, :], op0=ALU.mult, op1=ALU.subtract),
                [bPB[5 + hh], btt[ti]], [bstat[1]])
            S_.act(lambda sl=sl: nc.scalar.activation(out=stat[:, 1, sl], in_=stat[:, 1, sl], func=AF.Ln,
                                                      bias=LN_EPS, scale=1.0), [bstat[1]], [bstat[1]])
            S_.act(lambda sl=sl: nc.scalar.activation(out=stat[:, 1, sl], in_=stat[:, 1, sl], func=AF.Exp,
                                                      scale=-0.5), [bstat[1]], [bstat[1]])

    lt_slots = [(tt_[0][:, :], btt[0]), (tt_[1][:, :], btt[1]), (sg[0][:, :], bsg[0]), (sg[1][:, :], bsg[1])]

    def ln_stage1(n, hh):
        sl = slice(hh * 512, (hh + 1) * 512)
        tb, btb = lt_slots[nxt("lt", 4)]
        S_.dve(lambda: nc.vector.tensor_tensor(out=tb, in0=RAf[:, n, sl], in1=stat[:, 0, sl], op=ALU.subtract),
               [bRAh[n][hh], bstat[0]], [btb])
        return (n, hh, tb, btb)

    def ln_stage2(ctx, gi, bi, outs):
        n, hh, tb, btb = ctx
        sl = slice(hh * 512, (hh + 1) * 512)
        S_.dve(lambda: nc.vector.tensor_tensor(out=tb, in0=tb, in1=stat[:, 1, sl], op=ALU.mult), [btb, bstat[1]], [btb])
        for (oap, bo) in outs:
            S_.act(lambda oap=oap: nc.scalar.activation(out=oap, in_=tb, func=AF.Identity,
                                                        bias=lnp[:, bi, n:n + 1], scale=lnp[:, gi, n:n + 1]),
                   [btb, bsmall], [bo])

    def ln_apply(n, hh, gi, bi, outs):
        ln_stage2(ln_stage1(n, hh), gi, bi, outs)

    def ln2_piece(n, hh, c0_):
        oi = nxt("of", 2)
        ln_apply(n, hh, 2, 3, [(of[oi][:, :], bof[oi])])
        bo = Buf("out%d_%d_%d" % (n, c0_, hh))
        out_ops.append(S_.dma("sp", lambda oi=oi: nc.sync.dma_start(
            out=outT_d[n, :, c0_ + hh * 512:c0_ + (hh + 1) * 512], in_=of[oi][:, :]), [bof[oi]], [bo], "of%d" % oi))

    def ln2_tail(ctx, c0_):
        n, hh = ctx[0], ctx[1]
        oi = nxt("of", 2)
        ln_stage2(ctx, 2, 3, [(of[oi][:, :], bof[oi])])
        bo = Buf("out%d_%d_%d" % (n, c0_, hh))
        out_ops.append(S_.dma("sp", lambda oi=oi: nc.sync.dma_start(
            out=outT_d[n, :, c0_ + hh * 512:c0_ + (hh + 1) * 512], in_=of[oi][:, :]), [bof[oi]], [bo], "of%d" % oi))

    ln2_pending = None
    for st in range(TOK // TS):
        c0 = st * TS
        for c in range(KC):
            S_.dma("sp", lambda c=c, c0=c0: nc.sync.dma_start(out=RD[:, c, :], in_=ys_d[c, :, c0:c0 + TS]),
                   [], [bRD[c]], "RD%d" % c)
        for g in range(2):
            for cc in range(8):
                c = g * 8 + cc
                for hh in range(2):
                    qi = nxt("sq", 2)
                    if g == 0 and (2 * cc + hh) % 2 == 1:
                        S_.dve(lambda c=c, qi=qi, hh=hh: nc.vector.tensor_tensor(
                            out=sq[qi][:, :], in0=RD[:, c, hh * 512:(hh + 1) * 512], in1=RD[:, c, hh * 512:(hh + 1) * 512],
                            op=ALU.mult), [bRD[c]], [bsq[qi]])
                    else:
                        S_.act(lambda c=c, qi=qi, hh=hh: nc.scalar.activation(
                            out=sq[qi][:, :], in_=RD[:, c, hh * 512:(hh + 1) * 512], func=AF.Square), [bRD[c]], [bsq[qi]])
                    S_.pe(lambda qi=qi, cc=cc, g=g, hh=hh: nc.tensor.matmul(
                        PB[3 + 2 * g + hh][:], ONES, sq[qi][:, :], start=(cc == 0), stop=(cc == 7)),
                        [bsq[qi], bcst], [bPB[3 + 2 * g + hh]])
            for hh in range(2):
                S_.act(lambda g=g, hh=hh: nc.scalar.activation(
                    out=PB[3 + 2 * g + hh][:], in_=PB[3 + 2 * g + hh][:], func=AF.Ln,
                    bias=1024.0 * RMS_EPS, scale=1.0), [bPB[3 + 2 * g + hh]], [bPB[3 + 2 * g + hh]])
                S_.act(lambda g=g, hh=hh: nc.scalar.activation(
                    out=PB[3 + 2 * g + hh][:], in_=PB[3 + 2 * g + hh][:], func=AF.Exp,
                    scale=-0.5), [bPB[3 + 2 * g + hh]], [bPB[3 + 2 * g + hh]])
            for c in range(g * 8, g * 8 + 8):
                for hh in range(2):
                    S_.dve(lambda c=c, hh=hh, g=g: nc.vector.scalar_tensor_tensor(
                        out=RD[:, c, hh * 512:(hh + 1) * 512], in0=RD[:, c, hh * 512:(hh + 1) * 512], scalar=gmix[:, c:c + 1],
                        in1=PB[3 + 2 * g + hh][:], op0=ALU.mult, op1=ALU.mult),
                        [bRD[c], bPB[3 + 2 * g + hh], bsmall], [bRD[c]])
        pend = None
        for n in range(KC):
            wi, wv = wslot("wo")
            S_.dma("pool", lambda wv=wv, n=n: nc.gpsimd.dma_start(out=wv, in_=wout_d[n]), [], [bwsl[wi]], "wsl%d" % wi)
            for hh in range(2):
                xi = nxt("xr", 2)
                S_.dma("sp", lambda xi=xi, n=n, c0=c0, hh=hh: nc.sync.dma_start(
                    out=xr[xi][:, :], in_=xres_d[n, :, c0 + hh * 512:c0 + (hh + 1) * 512]), [], [bxr[xi]], "xr%d" % xi)

                rb = (2 * n + hh) % 3

                def mm(wv=wv, hh=hh, rb=rb):
                    ins = None
                    for kc in range(KC):
                        ins = nc.tensor.matmul(PB[rb][:], wv[:, kc, :], RD[:, kc, hh * 512:(hh + 1) * 512],
                                               start=(kc == 0), stop=(kc == KC - 1))
                    return ins
                S_.pe(mm, [bwsl[wi]] + bRD[0:KC], [bPB[rb]])
                if ln2_pending is not None:
                    ln2_piece(n, hh, ln2_pending)
                S_.dve(lambda hh=hh, n=n, xi=xi, rb=rb: nc.vector.scalar_tensor_tensor(
                    out=RAf[:, n, hh * 512:(hh + 1) * 512], in0=xr[xi][:, :], scalar=ALPHA,
                    in1=PB[rb][:], op0=ALU.mult, op1=ALU.add), [bxr[xi], bPB[rb]], [bRAh[n][hh]])
                if pend is not None:
                    stats_accum(pend[0], pend[1], pend[0] == 0, pend[0] == KC - 1)
                pend = (n, hh)
        stats_accum(pend[0], pend[1], pend[0] == 0, pend[0] == KC - 1)
        ln2_pending = None
        ln_finish()
        def ln1_s1(n, hh):
            k_ = nxt("nt", 4)
            tbuf, btb = nt_slots[k_]
            sl = slice(hh * 512, (hh + 1) * 512)
            S_.dve(lambda: nc.vector.tensor_tensor(out=tbuf, in0=RAf[:, n, sl], in1=stat[:, 0, sl], op=ALU.subtract),
                   [bRAh[n][hh], bstat[0]], [btb])
            return (n, hh, k_, tbuf, btb, sl)

        def ln1_s2(ctx, st=st, c0=c0):
            n, hh, k_, tbuf, btb, sl = ctx
            S_.dve(lambda: nc.vector.tensor_tensor(out=tbuf, in0=tbuf, in1=stat[:, 1, sl], op=ALU.mult),
                   [btb, bstat[1]], [btb])
            S_.act(lambda: nc.scalar.activation(out=x1b(n, hh), in_=tbuf, func=AF.Identity,
                                                bias=lnp[:, 1, n:n + 1], scale=lnp[:, 0, n:n + 1]),
                   [btb, bsmall], [bRAh[n // 2][hh]])
            bx = Buf("x1s_%d_%d_%d" % (n, st, hh))
            bx1s[(n, st, hh)] = bx
            S_.dma("sp", lambda: nc.sync.dma_start(out=x1s_d[n, :, c0 + hh * 512:c0 + (hh + 1) * 512], in_=tbuf),
                   [btb], [bx], "nts%d" % k_)

        prev_ = None
        for hh in range(2):
            for n in range(KC):
                ctx = ln1_s1(n, hh)
                if prev_ is not None:
                    ln1_s2(prev_)
                prev_ = ctx
        ln1_s2(prev_)
        order = [(0, 0), (1, 0), (0, 1), (1, 1)] + [(f, hh) for f in range(2, FC) for hh in range(2)]
        wcur = {}
        ucnt = 0
        for (f, hh) in order:
            if f not in wcur:
                wi, wv = wslot("wu")
                S_.dma("pool", lambda wv=wv, f=f: nc.gpsimd.dma_start(out=wv, in_=wup_d[f]), [], [bwsl[wi]], "wsl%d" % wi)
                wcur[f] = (wi, wv)
            wi, wv = wcur[f]
            a = 2 * (ucnt % 3)
            ucnt += 1
            gb, ub = PB[a], PB[a + 1]

            def mm(wv=wv, hh=hh, gb=gb, ub=ub):
                ins = None
                for kc in range(KC):
                    nc.tensor.matmul(gb[:], wv[:, kc, 0:128], x1b(kc, hh), start=(kc == 0), stop=(kc == KC - 1))
                for kc in range(KC):
                    ins = nc.tensor.matmul(ub[:], wv[:, kc, 128:256], x1b(kc, hh), start=(kc == 0), stop=(kc == KC - 1))
                return ins
            S_.pe(mm, [bwsl[wi]] + [bRAh[c_][hh] for c_ in range(KC // 2)], [bPB[a], bPB[a + 1]])
            si = nxt("sg", 2)
            S_.act(lambda si=si, gb=gb: nc.scalar.activation(out=sg[si][:, :], in_=gb[:], func=AF.Silu),
                   [bPB[a]], [bsg[si]])
            S_.dve(lambda si=si, ub=ub, f=f, hh=hh: nc.vector.tensor_tensor(
                out=RD[:, f, hh * 512:(hh + 1) * 512], in0=ub[:], in1=sg[si][:, :], op=ALU.mult),
                [bPB[a + 1], bsg[si]], [bRD[f]])
        pend = None
        for n in range(KC):
            wi, wv = wslot("wd")
            S_.dma("pool", lambda wv=wv, n=n: nc.gpsimd.dma_start(out=wv, in_=wdn_d[n]), [], [bwsl[wi]], "wsl%d" % wi)
            for hh in range(2):
                xi = nxt("xr", 2)
                S_.dma("sp", lambda xi=xi, n=n, c0=c0, hh=hh: nc.sync.dma_start(
                    out=xr[xi][:, :], in_=x1s_d[n, :, c0 + hh * 512:c0 + (hh + 1) * 512]),
                    [bx1s[(n, st, hh)]], [bxr[xi]], "xr%d" % xi)
                S_.act(lambda xi=xi, n=n: nc.scalar.activation(out=xr[xi][:, :], in_=xr[xi][:, :], func=AF.Identity,
                                                               bias=lnp[:, 1, n:n + 1], scale=lnp[:, 0, n:n + 1]),
                       [bxr[xi], bsmall], [bxr[xi]])

                rb = (2 * n + hh) % 3

                def mm(wv=wv, hh=hh, rb=rb):
                    ins = None
                    for fc in range(FC):
                        ins = nc.tensor.matmul(PB[rb][:], wv[:, fc, :], RD[:, fc, hh * 512:(hh + 1) * 512],
                                               start=(fc == 0), stop=(fc == FC - 1))
                    return ins
                S_.pe(mm, [bwsl[wi]] + bRD, [bPB[rb]])
                S_.dve(lambda hh=hh, n=n, xi=xi, rb=rb: nc.vector.scalar_tensor_tensor(
                    out=RAf[:, n, hh * 512:(hh + 1) * 512], in0=xr[xi][:, :], scalar=ALPHA,
                    in1=PB[rb][:], op0=ALU.mult, op1=ALU.add), [bxr[xi], bPB[rb]], [bRAh[n][hh]])
                if pend is not None:
                    stats_accum(pend[0], pend[1], pend[0] == 0, pend[0] == KC - 1)
                pend = (n, hh)
        stats_accum(pend[0], pend[1], pend[0] == 0, pend[0] == KC - 1)
        ln_finish()
        if st + 1 < TOK // TS:
            ln2_pending = c0
        else:
            prev_ = None
            for n in range(KC):
                for hh in range(2):
                    ctx = ln_stage1(n, hh)
                    if prev_ is not None:
                        ln2_tail(prev_, c0)
                    prev_ = ctx
            ln2_tail(prev_, c0)
    return out_ops


def _rope_tables(pos, hd, nheads_in_chunk):
    half = hd // 2
    inv = 10000.0 ** (-(np.arange(half, dtype=np.float64) / float(half)))
    ang = pos.astype(np.float64)[None, :] * inv[:, None]
    c = np.cos(ang).astype(np.float32)
    s = np.sin(ang).astype(np.float32)
    cos_h = np.concatenate([c, c], 0)
    sin_h = np.concatenate([-s, s], 0)
    cos_t = np.concatenate([cos_h] * nheads_in_chunk, 0)
    sin_t = np.concatenate([sin_h] * nheads_in_chunk, 0)
    return np.stack([cos_t, sin_t], 1)


def _masks(core):
    p = np.arange(128)[:, None]
    f = np.arange(128)[None, :]
    z = lambda c: np.where(c, 1.0, 0.0).astype(np.float32)
    cur = z(p <= f)
    prevA = z(p >= f)
    prevB = z(p > f)
    allneg = np.zeros((128, 128), np.float32)
    haloA = prevA if core > 0 else allneg
    haloB = prevB if core > 0 else allneg
    t4 = lambda *ms: np.concatenate(ms, 1)
    il4 = lambda m: np.repeat(m, 4, axis=1)
    mb = np.zeros((128, NMB, 512), np.float32)
    mb[:, 0] = t4(prevA, prevA, prevA, prevA)
    mb[:, 1] = t4(haloA, prevA, prevA, prevA)
    mb[:, 2] = t4(cur, cur, cur, cur)
    mb[:, 3] = il4(prevA)
    mb[:, 4] = il4(haloA)
    mb[:, 5] = il4(cur)
    for k in range(4):
        mb[:, 6 + k] = np.repeat(haloA[:, 32 * k:32 * k + 32], 16, axis=1)
        mb[:, 10 + k] = np.repeat(cur[:, 32 * k:32 * k + 32], 16, axis=1)
    mb[:, 14] = t4(prevB, prevB, prevB, prevB)
    mb[:, 15] = t4(haloB, prevB, prevB, prevB)
    return mb


def _consts():
    c = np.zeros((128, 6, 128), np.float32)
    c[:, 0] = np.eye(128)
    pa = np.zeros((128, 128), np.float32)
    for i in range(128):
        pa[i, (i + 64) % 128] = 1.0
    c[:, 1] = pa
    pb = np.zeros((128, 128), np.float32)
    for i in range(128):
        base = (i // 64) * 64
        pb[i, base + ((i - base) + 32) % 64] = 1.0
    c[:, 2] = pb
    c[:, 3] = 1.0
    c[:, 4, 0:64] = 1.0
    c[:, 5, 64:128] = 1.0
    return c


def prepare_inputs(x, w_in, b_in, sinks, g_mix_a, g_mix_b, w_out, ln1_g, ln1_b, w_up, w_down, ln2_g, ln2_b):
    f = np.float32
    x = np.asarray(x, f)[0]
    w_in = np.asarray(w_in, f)[0]
    b_in = np.asarray(b_in, f)[0]
    sinks = np.asarray(sinks, f)[0]
    w_out = np.asarray(w_out, f)[0]
    w_up = np.asarray(w_up, f)[0]
    w_down = np.asarray(w_down, f)[0]
    cols = []
    for ch in range(32):
        cols.append(np.arange(ch * 128, ch * 128 + 128))
    kb0 = 4096 + np.arange(64)
    kb1 = 4096 + 64 + np.arange(64)
    cols.append(np.concatenate([kb0, kb0]))
    cols.append(np.concatenate([kb1, kb1]))
    cols.append(4096 + 128 + np.arange(128))
    cols = np.stack(cols)
    wsel = w_in[:, cols]
    win = np.ascontiguousarray(wsel.reshape(KC, 128, NCH_IN, 128).transpose(2, 1, 0, 3))
    binr = np.ascontiguousarray(b_in[cols].T)
    wout = np.ascontiguousarray(w_out.reshape(KC, 128, KC, 128).transpose(2, 1, 0, 3))
    gate = w_up[:, :DFF].reshape(KC, 128, FC, 128)
    up = w_up[:, DFF:].reshape(KC, 128, FC, 128)
    wup = np.ascontiguousarray(np.concatenate([gate, up], -1).transpose(2, 1, 0, 3))
    wdn = np.ascontiguousarray(w_down.reshape(FC, 128, KC, 128).transpose(2, 1, 0, 3))
    gm = np.concatenate([np.asarray(g_mix_a, f)[0], np.asarray(g_mix_b, f)[0]])
    gmix = np.ascontiguousarray(gm.reshape(KC, 128).T)
    lnp = np.ascontiguousarray(np.stack([np.asarray(a, f)[0].reshape(KC, 128).T
                                         for a in (ln1_g, ln1_b, ln2_g, ln2_b)], 1))
    sk = np.zeros((128, 8), f)
    for j in range(8):
        sk[0:64, j] = sinks[2 * j]
        sk[64:128, j] = sinks[2 * j + 1]
    consts = _consts()
    shared = dict(win=win, bin=binr, wout=wout, wup=wup, wdn=wdn, gmix=gmix, lnp=lnp, sinks=sk, consts=consts)
    in_maps = []
    xpad = np.concatenate([np.zeros((TOK, D), f), x], 0)
    for c in range(NCORES):
        t0 = c * TOK
        xe = xpad[t0:t0 + 2 * TOK]
        xTe = xe.T.reshape(KC, 128, 2, TOK)
        xT = np.ascontiguousarray(xTe.transpose(2, 1, 0, 3))
        xres = np.ascontiguousarray(xe[TOK:].T.reshape(KC, 128, TOK))
        pos = np.arange(t0 - TOK, t0 + TOK)
        rA = _rope_tables(pos, 128, 1).reshape(128, 2, 2, TOK).transpose(2, 0, 1, 3)
        rB = _rope_tables(pos, 64, 2).reshape(128, 2, 2, TOK).transpose(2, 0, 1, 3)
        m = dict(shared)
        m.update(xT=xT, xres=xres, ropeA=np.ascontiguousarray(rA), ropeB=np.ascontiguousarray(rB), maskb=_masks(c))
        in_maps.append(m)
    return in_maps


_NC_CACHE = {}


def kernel(**inputs):
    in_maps = prepare_inputs(**inputs)
    if "nc" not in _NC_CACHE:
        _NC_CACHE["nc"] = build()
    nc = _NC_CACHE["nc"]
    res = run_bass_kernel_spmd(nc, in_maps, core_ids=list(range(NCORES)))
    outs = []
    for c in range(NCORES):
        oT = res.results[c]["outT"]
        outs.append(oT.reshape(D, TOK).T)
    out = np.concatenate(outs, 0)[None].astype(np.float32)
    return np.ascontiguousarray(out)
```

```python
import math
from contextlib import ExitStack

import numpy as np
import concourse.bass as bass
import concourse.mybir as mybir
from concourse.bass_utils import run_bass_kernel_spmd

F32 = mybir.dt.float32
BF16 = mybir.dt.bfloat16
AF = mybir.ActivationFunctionType
ALU = mybir.AluOpType

NCORES = 8
S = 16384
D = 2048
TOK = S // NCORES
KC = D // 128
DFF = 5632
FC = DFF // 128
NCH_IN = 35
ALPHA = 2.0 ** 0.25
LN_EPS = 1e-5
RMS_EPS = 1e-6
NEG = -30000.0
NMB = 16


class Buf:
    __slots__ = ("name", "w", "r", "rd")

    def __init__(self, name):
        self.name = name
        self.w = None
        self.r = {}
        self.rd = []


class Op:
    __slots__ = ("eng", "fn", "deps", "is_dma", "semkey", "val", "needs_inc")


class Sched:
    def __init__(self, nc, stack):
        self.nc = nc
        self.stack = stack
        self.ops = []
        self.dma_cnt = {}
        self.bar_deps = set()
        self.bar_done = set()
        self.dma_since_bar = []
        self.last = {}

    def _add(self, eng, fn, reads, writes, semkey=None):
        idx = len(self.ops)
        deps = set()
        for b in reads:
            if b.w is not None:
                deps.add(b.w)
        for b in writes:
            if b.w is not None:
                deps.add(b.w)
            deps.update(b.r.values())
            deps.update(b.rd)
        if eng not in self.bar_done:
            deps |= self.bar_deps
            self.bar_done.add(eng)
        op = Op()
        op.eng = eng
        op.fn = fn
        op.deps = deps
        op.is_dma = semkey is not None
        op.semkey = semkey
        op.val = 0
        op.needs_inc = False
        if op.is_dma:
            c = self.dma_cnt.get(semkey, 0) + 16
            self.dma_cnt[semkey] = c
            op.val = c
            self.dma_since_bar.append(idx)
        self.ops.append(op)
        for b in reads:
            if op.is_dma:
                b.rd.append(idx)
            else:
                b.r[eng] = idx
        for b in writes:
            b.w = idx
            b.r = {}
            b.rd = []
        self.last[eng] = idx
        return idx

    def pe(self, fn, reads, writes):
        return self._add("pe", fn, reads, writes)

    def act(self, fn, reads, writes):
        return self._add("act", fn, reads, writes)

    def dve(self, fn, reads, writes):
        return self._add("dve", fn, reads, writes)

    def dma(self, q, fn, reads, writes, semkey):
        return self._add(q, fn, reads, writes, semkey=semkey)

    def barrier(self):
        deps = set(self.dma_since_bar)
        for e, i in self.last.items():
            deps.add(i)
        self.bar_deps = self.bar_deps | deps
        self.bar_done = set()
        self.dma_since_bar = []

    def emit(self, final_wait_ops):
        nc = self.nc
        engs = {"pe": nc.tensor, "act": nc.scalar, "dve": nc.vector, "pool": nc.gpsimd, "sp": nc.sync}
        for op in self.ops:
            for d in op.deps:
                dop = self.ops[d]
                if not dop.is_dma:
                    if dop.eng == op.eng and op.eng == "pe":
                        continue
                    dop.needs_inc = True
        cnt = {e: 0 for e in engs}
        for op in self.ops:
            if (not op.is_dma) and op.needs_inc:
                cnt[op.eng] += 1
                op.val = cnt[op.eng]
        esem = {e: self.stack.enter_context(nc.semaphore("e_" + e)) for e in ("pe", "act", "dve")}
        dsem = {}
        for k in self.dma_cnt:
            dsem[k] = self.stack.enter_context(nc.semaphore("d_" + str(k)))
        waited = {e: {} for e in engs}
        for op in self.ops:
            stream = engs[op.eng]
            need = {}
            for d in op.deps:
                dop = self.ops[d]
                if dop.is_dma:
                    key = ("d", dop.semkey)
                else:
                    if dop.eng == op.eng and op.eng == "pe":
                        continue
                    key = ("e", dop.eng)
                if dop.val > need.get(key, 0):
                    need[key] = dop.val
            w = waited[op.eng]
            for key, val in need.items():
                if w.get(key, 0) < val:
                    sem = dsem[key[1]] if key[0] == "d" else esem[key[1]]
                    stream.wait_ge(sem, val)
                    w[key] = val
            ins = op.fn()
            if op.is_dma:
                ins.then_inc(dsem[op.semkey], 16)
            elif op.needs_inc:
                ins.then_inc(esem[op.eng], 1)
        for i in final_wait_ops:
            op = self.ops[i]
            nc.sync.wait_ge(dsem[op.semkey], op.val)


def build(debug=False):
    nc = bass.Bass("TRN2", target_bir_lowering=False)
    dt = nc.dram_tensor
    xT_d = dt("xT", [2, 128, KC, TOK], F32, kind="ExternalInput").ap()
    xres_d = dt("xres", [KC, 128, TOK], F32, kind="ExternalInput").ap()
    ropeA_d = dt("ropeA", [2, 128, 2, TOK], F32, kind="ExternalInput").ap()
    ropeB_d = dt("ropeB", [2, 128, 2, TOK], F32, kind="ExternalInput").ap()
    maskb_d = dt("maskb", [128, NMB, 512], F32, kind="ExternalInput").ap()
    consts_d = dt("consts", [128, 6, 128], F32, kind="ExternalInput").ap()
    win_d = dt("win", [NCH_IN, 128, KC, 128], F32, kind="ExternalInput").ap()
    bin_d = dt("bin", [128, NCH_IN], F32, kind="ExternalInput").ap()
    wout_d = dt("wout", [KC, 128, KC, 128], F32, kind="ExternalInput").ap()
    wup_d = dt("wup", [FC, 128, KC, 256], F32, kind="ExternalInput").ap()
    wdn_d = dt("wdn", [KC, 128, FC, 128], F32, kind="ExternalInput").ap()
    gmix_d = dt("gmix", [128, KC], F32, kind="ExternalInput").ap()
    lnp_d = dt("lnp", [128, 4, KC], F32, kind="ExternalInput").ap()
    sinks_d = dt("sinks", [128, 8], F32, kind="ExternalInput").ap()
    outT_d = dt("outT", [KC, 128, TOK], F32, kind="ExternalOutput").ap()
    kvs_d = dt("kvs", [19, 128, 2 * TOK], BF16, kind="Internal").ap()
    ys_d = dt("ys", [KC, 128, TOK], BF16, kind=("ExternalOutput" if debug else "Internal")).ap()
    x1s_d = dt("x1s", [KC, 128, TOK], F32, kind="Internal").ap()

    stack = ExitStack()
    with stack:
        S_ = Sched(nc, stack)
        ARENA_EL = 106400
        arena = stack.enter_context(nc.sbuf_tensor("arena", [128, ARENA_EL], BF16))
        apos = {"o": 0}

        def sb(name, shape, dty):
            n = 1
            for s_ in shape[1:]:
                n *= s_
            nel = n * (2 if dty == F32 else 1)
            nel = (nel + 15) // 16 * 16
            o = apos["o"]
            assert o + nel <= ARENA_EL, (name, o, nel)
            apos["o"] = o + nel
            v = arena[:, o:o + nel]
            if dty == F32:
                v = v.bitcast(F32)
            v = v[:, 0:n]
            if len(shape) == 3:
                v = v.rearrange("p (a b) -> p a b", a=shape[1])
            return v
        PB = [stack.enter_context(nc.psum_tensor("pb%d" % i, [128, 512], F32)) for i in range(7)]
        PT = stack.enter_context(nc.psum_tensor("pt", [128, 1024], BF16))
        bPB = [Buf("pb%d" % i) for i in range(7)]
        bPT = Buf("pt")

        cst = sb("cst", [128, 6, 128], BF16)
        bcst = Buf("cst")
        IDENT, PERMA, PERMB, ONES, ONESL, ONESR = (cst[:, i, :] for i in range(6))
        binb = sb("binb", [128, NCH_IN], F32)
        gmix = sb("gmix", [128, KC], F32)
        lnp = sb("lnp", [128, 4, KC], F32)
        esink = sb("esink", [128, 8], F32)
        bsmall = Buf("small")
        S_.dma("pool", lambda: nc.gpsimd.dma_start(out=cst[:], in_=consts_d[:, :, :]), [], [bcst], "cst")
        S_.dma("sp", lambda: nc.sync.dma_start(out=binb[:], in_=bin_d[:, :]), [], [bsmall], "small")
        S_.dma("sp", lambda: nc.sync.dma_start(out=gmix[:], in_=gmix_d[:, :]), [], [bsmall], "small")
        S_.dma("sp", lambda: nc.sync.dma_start(out=lnp[:], in_=lnp_d[:, :, :]), [], [bsmall], "small")
        S_.dma("sp", lambda: nc.sync.dma_start(out=esink[:], in_=sinks_d[:, :]), [], [bsmall], "small")
        S_.act(lambda: nc.scalar.activation(out=esink[:], in_=esink[:], func=AF.Exp), [bsmall], [bsmall])
        S_.dve(lambda: nc.vector.tensor_scalar(out=gmix[:], in0=gmix[:], scalar1=32.0, scalar2=None,
                                               op0=ALU.mult), [bsmall], [bsmall])

        base_persist = apos["o"]
        if True:
            sb1 = sb
            xT = sb1("xT", [128, KC, TOK], BF16)
            bxT = Buf("xT")
            ropeA = sb1("ropeA", [128, 2, TOK], F32)
            ropeB = sb1("ropeB", [128, 2, TOK], F32)
            brope = Buf("rope")
            maskb = sb1("maskb", [128, NMB, 512], BF16)
            bmask = Buf("maskb")
            NW = 3
            wbuf = [sb1("w%d" % i, [128, KC, 128], BF16) for i in range(NW)]
            bw = [Buf("w%d" % i) for i in range(NW)]
            pre = [sb1("pre%d" % i, [128, 512], BF16) for i in range(2)]
            bpre = [Buf("pre%d" % i) for i in range(2)]
            t1 = [sb1("t1_%d" % i, [128, 512], F32) for i in range(2)]
            t2 = [sb1("t2_%d" % i, [128, 512], F32) for i in range(1)]
            bt1 = [Buf("t1_%d" % i) for i in range(2)]
            bt2 = [Buf("t2_%d" % i) for i in range(1)]
            kvout = [sb1("kvo%d" % i, [128, TOK], BF16) for i in range(2)]
            bkvout = [Buf("kvo%d" % i) for i in range(2)]
            o_halo = apos["o"]
            xTh = sb1("xTh", [128, KC, TOK], BF16)
            bxTh = Buf("xTh")


            state = {"w": 0, "pre": 0, "acc": 0}

            def load_w(ch):
                i = state["w"] % NW
                state["w"] += 1
                S_.dma("pool", lambda i=i, ch=ch: nc.gpsimd.dma_start(out=wbuf[i][:], in_=win_d[ch]),
                       [], [bw[i]], "w%d" % i)
                return i

            def proj_tile(wi, ch, tt, rope_tab, perm, dst_ap, bdst, xsrc=None, bxsrc=None):
                a = state["acc"] % 2
                state["acc"] += 1
                pb = PB[a]
                if xsrc is None:
                    xsrc, bxsrc = xT, bxT
                if isinstance(bxsrc, list):
                    bxsrc = bxsrc[tt]

                def mm(wi=wi, tt=tt, pb=pb, xsrc=xsrc):
                    ins = None
                    for kc in range(KC):
                        ins = nc.tensor.matmul(pb[:], wbuf[wi][:, kc, :], xsrc[:, kc, tt * 512:(tt + 1) * 512],
                                               start=(kc == 0), stop=(kc == KC - 1))
                    return ins
                S_.pe(mm, [bw[wi]] + (bxsrc if isinstance(bxsrc, list) else [bxsrc]), [bPB[a]])
                flush_rope()
                if rope_tab is None:
                    S_.act(lambda pb=pb, ch=ch: nc.scalar.activation(out=dst_ap, in_=pb[:], func=AF.Identity,
                                                                     bias=binb[:, ch:ch + 1], scale=1.0),
                           [bPB[a], bsmall], [bdst])
                    return
                pi = state["pre"] % 2
                state["pre"] += 1
                S_.act(lambda pb=pb, ch=ch, pi=pi: nc.scalar.activation(out=pre[pi][:], in_=pb[:], func=AF.Identity,
                                                                        bias=binb[:, ch:ch + 1], scale=1.0),
                       [bPB[a], bsmall], [bpre[pi]])
                cs = rope_tab[:, 0, tt * 512:(tt + 1) * 512]
                sn = rope_tab[:, 1, tt * 512:(tt + 1) * 512]

                def stage_b(pi=pi, cs=cs, sn=sn, dst_ap=dst_ap, bdst=bdst, perm=perm):
                    S_.pe(lambda: nc.tensor.matmul(PB[2][:], perm, pre[pi][:], start=True, stop=True),
                          [bpre[pi], bcst], [bPB[2]])
                    S_.dve(lambda: nc.vector.tensor_tensor(out=t1[pi][:], in0=pre[pi][:], in1=cs, op=ALU.mult),
                           [bpre[pi], brope], [bt1[pi]])
                    S_.dve(lambda: nc.vector.tensor_tensor(out=t2[0][:], in0=PB[2][:], in1=sn, op=ALU.mult),
                           [bPB[2], brope], [bt2[0]])
                    S_.dve(lambda: nc.vector.tensor_tensor(out=dst_ap, in0=t1[pi][:], in1=t2[0][:], op=ALU.add),
                           [bt1[pi], bt2[0]], [bdst])
                state["rope_pending"] = stage_b

            def flush_rope():
                f = state.get("rope_pending")
                if f is not None:
                    state["rope_pending"] = None
                    f()

            kv_chunks = list(range(8, 24)) + [32, 33, 34]
            kv_order = list(range(8, 16)) + [18] + list(range(0, 8)) + [16, 17]
            kvs_store_ops = []
            xT2d = xT.rearrange("p a b -> p (a b)")
            stg = [xT2d[:, i * 16384:(i + 1) * 16384].bitcast(F32).rearrange("p (a b) -> p a b", a=KC) for i in range(2)]
            bstg = [Buf("stg0"), Buf("stg1")]
            bxTh_t = [[Buf("xTh_t%d_a" % t_), Buf("xTh_t%d_b" % t_)] for t_ in range(4)]
            S_.dma("pool", lambda: nc.gpsimd.dma_start(out=xTh[:, :, 1536:2048], in_=xT_d[0, :, :, 1536:2048]),
                   [], [bxTh_t[3][0], bxTh_t[3][1]], "xTh3")
            for t_ in range(3):
                s_ = t_ % 2
                S_.dma("sp", lambda t_=t_, s_=s_: nc.sync.dma_start(out=stg[s_], in_=xT_d[0, :, :, t_ * 512:(t_ + 1) * 512]),
                       [], [bstg[s_]], "stg%d" % s_)
                S_.act(lambda t_=t_, s_=s_: nc.scalar.copy(out=xTh[:, 0:8, t_ * 512:(t_ + 1) * 512], in_=stg[s_][:, 0:8, :]),
                       [bstg[s_]], [bxTh_t[t_][0]])
                S_.dve(lambda t_=t_, s_=s_: nc.vector.tensor_copy(out=xTh[:, 8:16, t_ * 512:(t_ + 1) * 512],
                                                                  in_=stg[s_][:, 8:16, :]), [bstg[s_]], [bxTh_t[t_][1]])
            for half in range(2):
                xs_, bxs_ = (xTh, bxTh_t) if half == 0 else (xT, bxT)
                rdep = [bstg[0]] if half == 0 else []
                S_.dma("sp", lambda half=half: nc.sync.dma_start(out=ropeA[:], in_=ropeA_d[half]), rdep, [brope], "rope")
                S_.dma("sp", lambda half=half: nc.sync.dma_start(out=ropeB[:], in_=ropeB_d[half]), rdep, [brope], "rope")
                nxt = load_w(kv_chunks[kv_order[0]])
                for oi_, ci in enumerate(kv_order):
                    ch = kv_chunks[ci]
                    wi = nxt
                    if oi_ + 1 < len(kv_order):
                        nxt = load_w(kv_chunks[kv_order[oi_ + 1]])
                    o = oi_ % 2
                    if 8 <= ch < 16:
                        tab, perm = ropeA, PERMA
                    elif ch in (32, 33):
                        tab, perm = ropeB, PERMB
                    else:
                        tab, perm = None, None
                    for tt in range(4):
                        if half == 0 and ch >= 32 and tt < 3:
                            continue
                        proj_tile(wi, ch, tt, tab, perm, kvout[o][:, tt * 512:(tt + 1) * 512], bkvout[o], xs_, bxs_)
                    if half == 0 and oi_ == 17:
                        S_.dma("pool", lambda: nc.gpsimd.dma_start(out=maskb[:], in_=maskb_d[:, :, :]), [], [bmask], "maskb")
                    if half == 0 and 1 <= oi_ <= KC:
                        kc = oi_ - 1
                        S_.dma("pool", lambda kc=kc: nc.gpsimd.dma_start(out=xT[:, kc, :], in_=xT_d[1, :, kc, :]),
                               [], [bxT, bstg[kc // 8]], "xT")
                    flush_rope()
                    bkv = Buf("kvs_%d_%d" % (ci, half))
                    k = S_.dma("sp", lambda o=o, ci=ci, half=half: nc.sync.dma_start(
                        out=kvs_d[ci, :, half * TOK:(half + 1) * TOK], in_=kvout[o][:]),
                        [bkvout[o]], [bkv], "kvo%d" % o)
                    kvs_store_ops.append(k)
            S_.barrier()

            apos["o"] = o_halo
            if True:
                sb2 = sb
                qT = sb2("qT", [128, TOK], BF16)
                kT = sb2("kT", [128, 2 * TOK], BF16)
                vT = sb2("vT", [128, 2 * TOK], BF16)
                bq, bk, bv = Buf("qT"), Buf("kT"), Buf("vT")
                vblks = [sb2("vblk%d" % i, [128, 69, 128], BF16) for i in range(2)]
                bvblks = [Buf("vblk0"), Buf("vblk1")]
                pbat = [sb2("pbat%d" % i, [128, 512], BF16) for i in range(3)]
                bpbat = [Buf("pbat%d" % i) for i in range(3)]
                SBANK = [3, 4, 0]
                lsb = sb2("lsb", [128, 512], F32)
                blsb = Buf("lsb")
                yah = sb2("yah", [128, TOK], BF16)
                byah = Buf("yah")
                state["sb"] = 0

                def s_batch(tiles, sel, mask_idx, scale):
                    si = state["sb"] % 3
                    state["sb"] += 1
                    bank = PB[SBANK[si]]

                    def mm(tiles=tiles, bank=bank, mask_idx=mask_idx):
                        first = True
                        ins = None
                        nt = len(tiles)
                        for i, (l, r) in enumerate(tiles):
                            ins = nc.tensor.matmul(sel(bank[:, :], i), l, r, start=first, stop=(i == nt - 1),
                                                   skip_group_check=True)
                            first = False
                        return ins
                    S_.pe(mm, [bq, bk], [bPB[SBANK[si]]])
                    S_.act(lambda bank=bank, si=si: nc.scalar.activation(out=pbat[si][:], in_=bank[:], func=AF.Exp,
                                                                         scale=scale),
                           [bPB[SBANK[si]]], [bpbat[si]])
                    S_.dve(lambda si=si, mask_idx=mask_idx: nc.vector.tensor_tensor(
                        out=pbat[si][:], in0=pbat[si][:], in1=maskb[:, mask_idx, :], op=ALU.mult),
                        [bpbat[si], bmask], [bpbat[si]])
                    return si

                OL = [(5, 6), (1, 2)]

                def pv_batch(si, vtiles, sel, lones, oset, first_in_bank, last_in_bank, bvblk):
                    ob, lb = OL[oset]

                    def mm(si=si, vtiles=vtiles):
                        f = first_in_bank
                        nt = len(vtiles)
                        for i, va in enumerate(vtiles):
                            nc.tensor.matmul(sel(PB[ob][:, :], i), va, sel(pbat[si][:, :], i), start=f,
                                             stop=(last_in_bank and i == nt - 1), skip_group_check=True)
                            f = False
                        return nc.tensor.matmul(PB[lb][:], lones, pbat[si][:], start=first_in_bank,
                                                stop=last_in_bank, skip_group_check=True)
                    S_.pe(mm, [bpbat[si], bvblk, bcst], [bPB[ob], bPB[lb]])

                def run_pipeline(items, evac):
                    pend_ = []
                    for it in items + [None, None]:
                        if it is not None:
                            si = s_batch(it["tl"], it["sel"], it["mi"], it["scale"])
                            pend_.append((it, si))
                        if len(pend_) > 2 or (it is None and pend_):
                            pit, psi = pend_.pop(0)
                            pv_batch(psi, pit["vl"], pit["sel"], pit["lones"], (pit["k"] + 1) % 2, pit["first"], pit["last"], pit["bv"])
                            if pit["qend"]:
                                evac(pit["k"])

                PTs = [PT[:, :], PB[0][:, :].bitcast(BF16)]
                bPTh = [bPT, bPB[0]]
                state["pt"] = 0

                def transposes(src, col_lists, dst_base, vblk, bvblk):
                    i = 0
                    while i < len(col_lists):
                        grp = col_lists[i:i + 8]
                        hf = state["pt"] % 2
                        state["pt"] += 1

                        def mm(grp=grp, hf=hf):
                            ins = None
                            for j, cap in enumerate(grp):
                                ins = nc.tensor.transpose(PTs[hf][:, j * 128:(j + 1) * 128], cap, IDENT)
                            return ins
                        S_.pe(mm, [src[1], bcst], [bPTh[hf]])
                        n = len(grp)
                        S_.dve(lambda n=n, i=i, hf=hf: nc.vector.tensor_copy(
                            out=vblk[:, dst_base + i:dst_base + i + n, :],
                            in_=PTs[hf][:, 0:n * 128].rearrange("p (j d) -> p j d", d=128)), [bPTh[hf]], [bvblk])
                        i += 8

                def strided(t, base, res, step, n):
                    if step == 1:
                        return t[:, base + res:base + res + n]
                    return t[:, base:base + step * n].rearrange("p (m s) -> p s m", s=step)[:, res, :]

                ys_store_ops = []
                for h in range(8):
                    wi = load_w(h)
                    S_.dma("sp", lambda h=h: nc.sync.dma_start(out=kT[:], in_=kvs_d[h]), [], [bk], "kT")
                    S_.dma("sp", lambda h=h: nc.sync.dma_start(out=vT[:], in_=kvs_d[8 + h]), [], [bv], "vT")
                    for tt in range(4):
                        proj_tile(wi, h, tt, ropeA, PERMA, qT[:, tt * 512:(tt + 1) * 512], bq)
                    cols = []
                    for b in range(-1, 16):
                        cols.append(strided(vT, TOK + 128 * b, 0, 1, 128))
                    for c in range(4):
                        for j in range(-1, 4):
                            cols.append(strided(vT, TOK + 512 * j, c, 4, 128))
                    for r in range(16):
                        cols.append(strided(vT, 0, r, 16, 128))
                        cols.append(strided(vT, TOK, r, 16, 128))
                    vblk, bvblk = vblks[h % 2], bvblks[h % 2]
                    transposes((vT, bv), cols, 0, vblk, bvblk)
                    flush_rope()
                    scale = 1.0 / math.sqrt(128.0)
                    items = []
                    selp = lambda t, i: t[:, 128 * i:128 * i + 128]
                    sel2 = lambda t, i: t.rearrange("p (i c) -> p c i", c=4)[:, i, :]
                    sel3 = lambda t, i: t.rearrange("p (a r) -> p r a", r=16)[:, i, :]
                    for k in range(4):
                        batches = []
                        for prev in (True, False):
                            tl, vl = [], []
                            for qb in range(4 * k, 4 * k + 4):
                                kb = qb - 1 if prev else qb
                                tl.append((kT[:, TOK + 128 * kb:TOK + 128 * kb + 128], qT[:, 128 * qb:128 * qb + 128]))
                                vl.append(vblk[:, kb + 1, :])
                            batches.append((tl, vl, ((1 if k == 0 else 0) if prev else 2), selp))
                        for prev in (True, False):
                            tl, vl = [], []
                            for c in range(4):
                                j = k - 1 if prev else k
                                tl.append((strided(kT, TOK + 512 * j, c, 4, 128), strided(qT, 512 * k, c, 4, 128)))
                                vl.append(vblk[:, 17 + c * 5 + (j + 1), :])
                            batches.append((tl, vl, ((4 if k == 0 else 3) if prev else 5), sel2))
                        for prev in (True, False):
                            tl, vl = [], []
                            for r in range(16):
                                kbase = 0 if prev else TOK
                                tl.append((strided(kT, kbase, r, 16, 128), strided(qT, 512 * k, r, 16, 32)))
                                vl.append(vblk[:, 37 + r * 2 + (0 if prev else 1), :])
                            batches.append((tl, vl, ((6 + k) if prev else (10 + k)), sel3))
                        for bi, (tl, vl, mi, sel_) in enumerate(batches):
                            items.append(dict(tl=tl, vl=vl, sel=sel_, mi=mi, lones=ONES, scale=scale, first=(bi == 0),
                                              last=(bi == len(batches) - 1), k=k, qend=(bi == len(batches) - 1), bv=bvblk))

                    def evacA(k):
                        ob, lb = OL[(k + 1) % 2]
                        S_.act(lambda: nc.scalar.activation(out=lsb[:], in_=PB[lb][:], func=AF.Ln), [bPB[lb]], [blsb])
                        S_.act(lambda: nc.scalar.activation(out=lsb[:], in_=lsb[:], func=AF.Exp, scale=-1.0), [blsb], [blsb])
                        S_.dve(lambda k=k: nc.vector.tensor_tensor(out=yah[:, k * 512:(k + 1) * 512], in0=PB[ob][:],
                                                                   in1=lsb[:], op=ALU.mult),
                               [bPB[ob], blsb], [byah])
                    run_pipeline(items, evacA)
                    bys = Buf("ys%d" % h)
                    ys_store_ops.append(S_.dma("sp", lambda h=h: nc.sync.dma_start(out=ys_d[h], in_=yah[:]),
                                               [byah], [bys], "yah"))

                scaleb = 1.0 / math.sqrt(64.0)
                for j in range(8):
                    g = j // 4
                    vblk, bvblk = vblks[g], bvblks[g]
                    wi = load_w(24 + j)
                    if j % 4 == 0:
                        S_.dma("sp", lambda g=g: nc.sync.dma_start(out=kT[:], in_=kvs_d[16 + g]), [], [bk], "kT")
                        if j == 0:
                            S_.dma("sp", lambda: nc.sync.dma_start(out=vT[:], in_=kvs_d[18]), [], [bv], "vT")
                        S_.dve(lambda vblk=vblk: nc.vector.memset(vblk[:, 0:34, :], 0.0), [], [bvblk])
                        for b0 in range(-1, 16, 8):
                            bl = list(range(b0, min(b0 + 8, 16)))
                            hf = state["pt"] % 2
                            state["pt"] += 1

                            def mm(bl=bl, hf=hf):
                                ins = None
                                for jj, b in enumerate(bl):
                                    ins = nc.tensor.transpose(PTs[hf][:, jj * 128:(jj + 1) * 128],
                                                              vT[:, TOK + 128 * b:TOK + 128 * b + 128], IDENT)
                                return ins
                            S_.pe(mm, [bv, bcst], [bPTh[hf]])
                            n = len(bl)
                            ptv = PTs[hf][:, 0:n * 128].rearrange("p (j d) -> p j d", d=128)
                            S_.dve(lambda n=n, b0=b0, ptv=ptv, g=g, vblk=vblk: nc.vector.tensor_copy(
                                out=vblk[:, b0 + 1:b0 + 1 + n, 0:64], in_=ptv[:, :, g * 64:(g + 1) * 64]), [bPTh[hf]], [bvblk])
                            S_.dve(lambda n=n, b0=b0, ptv=ptv, g=g, vblk=vblk: nc.vector.tensor_copy(
                                out=vblk[:, 17 + b0 + 1:17 + b0 + 1 + n, 64:128], in_=ptv[:, :, g * 64:(g + 1) * 64]),
                                [bPTh[hf]], [bvblk])
                    for tt in range(4):
                        proj_tile(wi, 24 + j, tt, ropeB, PERMB, qT[:, tt * 512:(tt + 1) * 512], bq)
                    flush_rope()
                    items = []
                    selp = lambda t, i: t[:, 128 * i:128 * i + 128]
                    for k in range(4):
                        nb = 0
                        for e in range(2):
                            lo, hi = 64 * e, 64 * e + 64
                            for prev in (True, False):
                                tl, vl = [], []
                                for qb in range(4 * k, 4 * k + 4):
                                    kb = qb - 1 if prev else qb
                                    tl.append((kT[lo:hi, TOK + 128 * kb:TOK + 128 * kb + 128],
                                               qT[lo:hi, 128 * qb:128 * qb + 128]))
                                    vl.append(vblk[:, e * 17 + kb + 1, :])
                                mi = (15 if k == 0 else 14) if prev else 2
                                items.append(dict(tl=tl, vl=vl, sel=selp, mi=mi, lones=(ONESL if e == 0 else ONESR),
                                                  scale=scaleb, first=(nb == 0), last=(nb == 3), k=k, qend=(nb == 3), bv=bvblk))
                                nb += 1

                    def evacB(k, j=j):
                        ob, lb = OL[(k + 1) % 2]
                        S_.act(lambda: nc.scalar.activation(out=lsb[:], in_=PB[lb][:], func=AF.Ln, bias=esink[:, j:j + 1],
                                                            scale=1.0), [bPB[lb], bsmall], [blsb])
                        S_.act(lambda: nc.scalar.activation(out=lsb[:], in_=lsb[:], func=AF.Exp, scale=-1.0), [blsb], [blsb])
                        S_.dve(lambda k=k: nc.vector.tensor_tensor(out=yah[:, k * 512:(k + 1) * 512], in0=PB[ob][:],
                                                                   in1=lsb[:], op=ALU.mult),
                               [bPB[ob], blsb], [byah])
                    run_pipeline(items, evacB)
                    bys = Buf("ys%d" % (8 + j))
                    ys_store_ops.append(S_.dma("sp", lambda j=j: nc.sync.dma_start(out=ys_d[8 + j], in_=yah[:]),
                                               [byah], [bys], "yah"))
        S_.barrier()
        apos["o"] = base_persist

        final_ops = list(ys_store_ops)
        if not debug or debug == "full":
            final_ops = phase2(nc, S_, stack, locals())
        S_.emit(final_ops)
    return nc


def phase2(nc, S_, stack, L):
    PB, bPB = L["PB"], L["bPB"]
    ONES = L["ONES"]
    bcst, bsmall = L["bcst"], L["bsmall"]
    gmix, lnp = L["gmix"], L["lnp"]
    ys_d, xres_d, x1s_d, outT_d = L["ys_d"], L["xres_d"], L["x1s_d"], L["outT_d"]
    wout_d, wup_d, wdn_d = L["wout_d"], L["wup_d"], L["wdn_d"]
    TS = 1024
    sb = L["sb"]
    RA2 = sb("RA", [128, KC * 2 * TS], BF16)
    RA = RA2.rearrange("p (a b) -> p a b", a=KC)
    RAf = RA2.bitcast(F32).rearrange("p (a b) -> p a b", a=KC)
    def x1b(n, hh):
        o_ = (n // 2) * 2 * TS + hh * TS + (n % 2) * 512
        return RA2[:, o_:o_ + 512]
    bRAh = [[Buf("RA%d_%d" % (n, h_)) for h_ in range(2)] for n in range(KC)]
    bRA = [b_ for pr in bRAh for b_ in pr]
    RD = sb("RD", [128, FC, TS], BF16)
    bRD = [Buf("RD%d" % f) for f in range(FC)]
    stat = sb("stat", [128, 2, TS], F32)
    bstat = [Buf("stat0"), Buf("stat1")]
    sq = [sb("sq%d" % i, [128, 512], BF16) for i in range(2)]
    zb = [sb("zb%d" % i, [128, 512], BF16) for i in range(2)]
    bsq = [Buf("sq%d" % i) for i in range(2)]
    bzb = [Buf("zb%d" % i) for i in range(2)]
    xr = [sb("xr%d" % i, [128, 512], F32) for i in range(2)]
    bxr = [Buf("xr%d" % i) for i in range(2)]
    tt_ = [sb("tt%d" % i, [128, 512], F32) for i in range(2)]
    btt = [Buf("tt%d" % i) for i in range(2)]
    of = [sb("of%d" % i, [128, 512], F32) for i in range(2)]
    bof = [Buf("of%d" % i) for i in range(2)]
    sg = [sb("sg%d" % i, [128, 512], F32) for i in range(2)]
    bsg = [Buf("sg%d" % i) for i in range(2)]
    WSL = FC * 128
    wsl = [sb("wsl%d" % i, [128, WSL], BF16) for i in range(2)]
    bwsl = [Buf("wsl%d" % i) for i in range(2)]
    ctr = {"sq": 0, "zb": 0, "xr": 0, "tt": 0, "of": 0, "sg": 0, "w": 0, "nt": 0, "lt": 0, "ot": 0}
    nt_slots = [(tt_[0][:, :], btt[0]), (tt_[1][:, :], btt[1]), (of[0][:, :], bof[0]), (of[1][:, :], bof[1])]
    out_ops = []
    bx1s = {}

    def nxt(name, n):
        i = ctr[name] % n
        ctr[name] += 1
        return i

    def wslot(kind):
        i = nxt("w", 2)
        if kind == "wo":
            v = wsl[i][:, 0:KC * 128].rearrange("p (a b) -> p a b", a=KC)
        elif kind == "wu":
            v = wsl[i][:, 0:KC * 256].rearrange("p (a b) -> p a b", a=KC)
        else:
            v = wsl[i][:, 0:FC * 128].rearrange("p (a b) -> p a b", a=FC)
        return i, v

    def stats_accum(n, hh, first, last):
        src_ap = RAf[:, n, hh * 512:(hh + 1) * 512]
        zi = nxt("zb", 2)
        qi = nxt("sq", 2)
        S_.act(lambda: nc.scalar.copy(out=zb[zi][:, :], in_=src_ap), [bRAh[n][hh]], [bzb[zi]])
        S_.act(lambda: nc.scalar.activation(out=sq[qi][:, :], in_=src_ap, func=AF.Square), [bRAh[n][hh]], [bsq[qi]])

        def mm():
            nc.tensor.matmul(PB[3 + hh][:], ONES, zb[zi][:, :], start=first, stop=last)
            return nc.tensor.matmul(PB[5 + hh][:], ONES, sq[qi][:, :], start=first, stop=last)
        S_.pe(mm, [bzb[zi], bsq[qi], bcst], [bPB[3 + hh], bPB[5 + hh]])

    def ln_finish():
        for hh in range(2):
            sl = slice(hh * 512, (hh + 1) * 512)
            ti = nxt("tt", 2)
            S_.dve(lambda hh=hh, sl=sl: nc.vector.tensor_scalar(out=stat[:, 0, sl], in0=PB[3 + hh][:], scalar1=1.0 / D,
                                                               scalar2=None, op0=ALU.mult), [bPB[3 + hh]], [bstat[0]])
            S_.dve(lambda sl=sl, ti=ti: nc.vector.tensor_tensor(out=tt_[ti][:, :], in0=stat[:, 0, sl], in1=stat[:, 0, sl],
                                                                op=ALU.mult), [bstat[0]], [btt[ti]])
            S_.dve(lambda hh=hh, sl=sl, ti=ti: nc.vector.scalar_tensor_tensor(
                out=stat[:, 1, sl], in0=PB[5 + hh][:], scalar=1.0 / D, in1=tt_[ti][:, :], op0=ALU.mult, op1=ALU.subtract),
                [bPB[5 + hh], btt[ti]], [bstat[1]])
            S_.act(lambda sl=sl: nc.scalar.activation(out=stat[:, 1, sl], in_=stat[:, 1, sl], func=AF.Ln,
                                                      bias=LN_EPS, scale=1.0), [bstat[1]], [bstat[1]])
            S_.act(lambda sl=sl: nc.scalar.activation(out=stat[:, 1, sl], in_=stat[:, 1, sl], func=AF.Exp,
                                                      scale=-0.5), [bstat[1]], [bstat[1]])

    lt_slots = [(tt_[0][:, :], btt[0]), (tt_[1][:, :], btt[1]), (sg[0][:, :], bsg[0]), (sg[1][:, :], bsg[1])]

    def ln_stage1(n, hh):
        sl = slice(hh * 512, (hh + 1) * 512)
        tb, btb = lt_slots[nxt("lt", 4)]
        S_.dve(lambda: nc.vector.tensor_tensor(out=tb, in0=RAf[:, n, sl], in1=stat[:, 0, sl], op=ALU.subtract),
               [bRAh[n][hh], bstat[0]], [btb])
        return (n, hh, tb, btb)

    def ln_stage2(ctx, gi, bi, outs):
        n, hh, tb, btb = ctx
        sl = slice(hh * 512, (hh + 1) * 512)
        S_.dve(lambda: nc.vector.tensor_tensor(out=tb, in0=tb, in1=stat[:, 1, sl], op=ALU.mult), [btb, bstat[1]], [btb])
        for (oap, bo) in outs:
            S_.act(lambda oap=oap: nc.scalar.activation(out=oap, in_=tb, func=AF.Identity,
                                                        bias=lnp[:, bi, n:n + 1], scale=lnp[:, gi, n:n + 1]),
                   [btb, bsmall], [bo])

    def ln_apply(n, hh, gi, bi, outs):
        ln_stage2(ln_stage1(n, hh), gi, bi, outs)

    def ln2_piece(n, hh, c0_):
        oi = nxt("of", 2)
        ln_apply(n, hh, 2, 3, [(of[oi][:, :], bof[oi])])
        bo = Buf("out%d_%d_%d" % (n, c0_, hh))
        out_ops.append(S_.dma("sp", lambda oi=oi: nc.sync.dma_start(
            out=outT_d[n, :, c0_ + hh * 512:c0_ + (hh + 1) * 512], in_=of[oi][:, :]), [bof[oi]], [bo], "of%d" % oi))

    ot_slots = [(of[0][:, :], bof[0], "of0"), (of[1][:, :], bof[1], "of1"),
                (xr[0][:, :], bxr[0], "xo0"), (xr[1][:, :], bxr[1], "xo1")]

    def ln2_tail(ctx, c0_):
        n, hh = ctx[0], ctx[1]
        obuf, bob, key = ot_slots[nxt("ot", 4)]
        ln_stage2(ctx, 2, 3, [(obuf, bob)])
        bo = Buf("out%d_%d_%d" % (n, c0_, hh))
        out_ops.append(S_.dma("sp", lambda: nc.sync.dma_start(
            out=outT_d[n, :, c0_ + hh * 512:c0_ + (hh + 1) * 512], in_=obuf), [bob], [bo], key))

    ln2_pending = None
    for st in range(TOK // TS):
        c0 = st * TS
        for c in range(KC):
            S_.dma("sp", lambda c=c, c0=c0: nc.sync.dma_start(out=RD[:, c, :], in_=ys_d[c, :, c0:c0 + TS]),
                   [], [bRD[c]], "RD%d" % c)
        for g in range(2):
            for cc in range(8):
                c = g * 8 + cc
                for hh in range(2):
                    qi = nxt("sq", 2)
                    if g == 0 and (2 * cc + hh) % 2 == 1:
                        S_.dve(lambda c=c, qi=qi, hh=hh: nc.vector.tensor_tensor(
                            out=sq[qi][:, :], in0=RD[:, c, hh * 512:(hh + 1) * 512], in1=RD[:, c, hh * 512:(hh + 1) * 512],
                            op=ALU.mult), [bRD[c]], [bsq[qi]])
                    else:
                        S_.act(lambda c=c, qi=qi, hh=hh: nc.scalar.activation(
                            out=sq[qi][:, :], in_=RD[:, c, hh * 512:(hh + 1) * 512], func=AF.Square), [bRD[c]], [bsq[qi]])
                    S_.pe(lambda qi=qi, cc=cc, g=g, hh=hh: nc.tensor.matmul(
                        PB[3 + 2 * g + hh][:], ONES, sq[qi][:, :], start=(cc == 0), stop=(cc == 7)),
                        [bsq[qi], bcst], [bPB[3 + 2 * g + hh]])
            for hh in range(2):
                S_.act(lambda g=g, hh=hh: nc.scalar.activation(
                    out=PB[3 + 2 * g + hh][:], in_=PB[3 + 2 * g + hh][:], func=AF.Ln,
                    bias=1024.0 * RMS_EPS, scale=1.0), [bPB[3 + 2 * g + hh]], [bPB[3 + 2 * g + hh]])
                S_.act(lambda g=g, hh=hh: nc.scalar.activation(
                    out=PB[3 + 2 * g + hh][:], in_=PB[3 + 2 * g + hh][:], func=AF.Exp,
                    scale=-0.5), [bPB[3 + 2 * g + hh]], [bPB[3 + 2 * g + hh]])
            for c in range(g * 8, g * 8 + 8):
                for hh in range(2):
                    S_.dve(lambda c=c, hh=hh, g=g: nc.vector.scalar_tensor_tensor(
                        out=RD[:, c, hh * 512:(hh + 1) * 512], in0=RD[:, c, hh * 512:(hh + 1) * 512], scalar=gmix[:, c:c + 1],
                        in1=PB[3 + 2 * g + hh][:], op0=ALU.mult, op1=ALU.mult),
                        [bRD[c], bPB[3 + 2 * g + hh], bsmall], [bRD[c]])
        pend = None
        for n in range(KC):
            wi, wv = wslot("wo")
            S_.dma("pool", lambda wv=wv, n=n: nc.gpsimd.dma_start(out=wv, in_=wout_d[n]), [], [bwsl[wi]], "wsl%d" % wi)
            for hh in range(2):
                xi = nxt("xr", 2)
                S_.dma("sp", lambda xi=xi, n=n, c0=c0, hh=hh: nc.sync.dma_start(
                    out=xr[xi][:, :], in_=xres_d[n, :, c0 + hh * 512:c0 + (hh + 1) * 512]), [], [bxr[xi]], "xr%d" % xi)

                rb = (2 * n + hh) % 3

                def mm(wv=wv, hh=hh, rb=rb):
                    ins = None
                    for kc in range(KC):
                        ins = nc.tensor.matmul(PB[rb][:], wv[:, kc, :], RD[:, kc, hh * 512:(hh + 1) * 512],
                                               start=(kc == 0), stop=(kc == KC - 1))
                    return ins
                S_.pe(mm, [bwsl[wi]] + bRD[0:KC], [bPB[rb]])
                if ln2_pending is not None:
                    ln2_piece(n, hh, ln2_pending)
                S_.dve(lambda hh=hh, n=n, xi=xi, rb=rb: nc.vector.scalar_tensor_tensor(
                    out=RAf[:, n, hh * 512:(hh + 1) * 512], in0=xr[xi][:, :], scalar=ALPHA,
                    in1=PB[rb][:], op0=ALU.mult, op1=ALU.add), [bxr[xi], bPB[rb]], [bRAh[n][hh]])
                if pend is not None:
                    stats_accum(pend[0], pend[1], pend[0] == 0, pend[0] == KC - 1)
                pend = (n, hh)
        stats_accum(pend[0], pend[1], pend[0] == 0, pend[0] == KC - 1)
        ln2_pending = None
        ln_finish()
        for hh in range(2):
            for n in range(KC):
                k_ = nxt("nt", 4)
                tbuf, btb = nt_slots[k_]
                sl = slice(hh * 512, (hh + 1) * 512)
                S_.dve(lambda n=n, sl=sl, tbuf=tbuf: nc.vector.tensor_tensor(out=tbuf, in0=RAf[:, n, sl], in1=stat[:, 0, sl],
                                                                             op=ALU.subtract),
                       [bRAh[n][hh], bstat[0]], [btb])
                S_.dve(lambda sl=sl, tbuf=tbuf: nc.vector.tensor_tensor(out=tbuf, in0=tbuf, in1=stat[:, 1, sl], op=ALU.mult),
                       [btb, bstat[1]], [btb])
                S_.act(lambda n=n, hh=hh, tbuf=tbuf: nc.scalar.activation(out=x1b(n, hh), in_=tbuf, func=AF.Identity,
                                                                          bias=lnp[:, 1, n:n + 1], scale=lnp[:, 0, n:n + 1]),
                       [btb, bsmall], [bRAh[n // 2][hh]])
                bx = Buf("x1s_%d_%d_%d" % (n, st, hh))
                bx1s[(n, st, hh)] = bx
                S_.dma("sp", lambda n=n, c0=c0, hh=hh, tbuf=tbuf: nc.sync.dma_start(
                    out=x1s_d[n, :, c0 + hh * 512:c0 + (hh + 1) * 512], in_=tbuf), [btb], [bx], "nts%d" % k_)
        order = [(0, 0), (1, 0), (0, 1), (1, 1)] + [(f, hh) for f in range(2, FC) for hh in range(2)]
        wcur = {}
        ucnt = 0
        for (f, hh) in order:
            if f not in wcur:
                wi, wv = wslot("wu")
                S_.dma("pool", lambda wv=wv, f=f: nc.gpsimd.dma_start(out=wv, in_=wup_d[f]), [], [bwsl[wi]], "wsl%d" % wi)
                wcur[f] = (wi, wv)
            wi, wv = wcur[f]
            a = 2 * (ucnt % 3)
            ucnt += 1
            gb, ub = PB[a], PB[a + 1]

            def mm(wv=wv, hh=hh, gb=gb, ub=ub):
                ins = None
                for kc in range(KC):
                    nc.tensor.matmul(gb[:], wv[:, kc, 0:128], x1b(kc, hh), start=(kc == 0), stop=(kc == KC - 1))
                for kc in range(KC):
                    ins = nc.tensor.matmul(ub[:], wv[:, kc, 128:256], x1b(kc, hh), start=(kc == 0), stop=(kc == KC - 1))
                return ins
            S_.pe(mm, [bwsl[wi]] + [bRAh[c_][hh] for c_ in range(KC // 2)], [bPB[a], bPB[a + 1]])
            si = nxt("sg", 2)
            S_.act(lambda si=si, gb=gb: nc.scalar.activation(out=sg[si][:, :], in_=gb[:], func=AF.Silu),
                   [bPB[a]], [bsg[si]])
            S_.dve(lambda si=si, ub=ub, f=f, hh=hh: nc.vector.tensor_tensor(
                out=RD[:, f, hh * 512:(hh + 1) * 512], in0=ub[:], in1=sg[si][:, :], op=ALU.mult),
                [bPB[a + 1], bsg[si]], [bRD[f]])
        pend = None
        for n in range(KC):
            wi, wv = wslot("wd")
            S_.dma("pool", lambda wv=wv, n=n: nc.gpsimd.dma_start(out=wv, in_=wdn_d[n]), [], [bwsl[wi]], "wsl%d" % wi)
            for hh in range(2):
                xi = nxt("xr", 2)
                S_.dma("sp", lambda xi=xi, n=n, c0=c0, hh=hh: nc.sync.dma_start(
                    out=xr[xi][:, :], in_=x1s_d[n, :, c0 + hh * 512:c0 + (hh + 1) * 512]),
                    [bx1s[(n, st, hh)]], [bxr[xi]], "xr%d" % xi)
                S_.act(lambda xi=xi, n=n: nc.scalar.activation(out=xr[xi][:, :], in_=xr[xi][:, :], func=AF.Identity,
                                                               bias=lnp[:, 1, n:n + 1], scale=lnp[:, 0, n:n + 1]),
                       [bxr[xi], bsmall], [bxr[xi]])

                rb = (2 * n + hh) % 3

                def mm(wv=wv, hh=hh, rb=rb):
                    ins = None
                    for fc in range(FC):
                        ins = nc.tensor.matmul(PB[rb][:], wv[:, fc, :], RD[:, fc, hh * 512:(hh + 1) * 512],
                                               start=(fc == 0), stop=(fc == FC - 1))
                    return ins
                S_.pe(mm, [bwsl[wi]] + bRD, [bPB[rb]])
                S_.dve(lambda hh=hh, n=n, xi=xi, rb=rb: nc.vector.scalar_tensor_tensor(
                    out=RAf[:, n, hh * 512:(hh + 1) * 512], in0=xr[xi][:, :], scalar=ALPHA,
                    in1=PB[rb][:], op0=ALU.mult, op1=ALU.add), [bxr[xi], bPB[rb]], [bRAh[n][hh]])
                if pend is not None:
                    stats_accum(pend[0], pend[1], pend[0] == 0, pend[0] == KC - 1)
                pend = (n, hh)
        stats_accum(pend[0], pend[1], pend[0] == 0, pend[0] == KC - 1)
        ln_finish()
        if st + 1 < TOK // TS:
            ln2_pending = c0
        else:
            prev_ = None
            for n in range(KC):
                for hh in range(2):
                    ctx = ln_stage1(n, hh)
                    if prev_ is not None:
                        ln2_tail(prev_, c0)
                    prev_ = ctx
            ln2_tail(prev_, c0)
    return out_ops


def _rope_tables(pos, hd, nheads_in_chunk):
    half = hd // 2
    inv = 10000.0 ** (-(np.arange(half, dtype=np.float64) / float(half)))
    ang = pos.astype(np.float64)[None, :] * inv[:, None]
    c = np.cos(ang).astype(np.float32)
    s = np.sin(ang).astype(np.float32)
    cos_h = np.concatenate([c, c], 0)
    sin_h = np.concatenate([-s, s], 0)
    cos_t = np.concatenate([cos_h] * nheads_in_chunk, 0)
    sin_t = np.concatenate([sin_h] * nheads_in_chunk, 0)
    return np.stack([cos_t, sin_t], 1)


def _masks(core):
    p = np.arange(128)[:, None]
    f = np.arange(128)[None, :]
    z = lambda c: np.where(c, 1.0, 0.0).astype(np.float32)
    cur = z(p <= f)
    prevA = z(p >= f)
    prevB = z(p > f)
    allneg = np.zeros((128, 128), np.float32)
    haloA = prevA if core > 0 else allneg
    haloB = prevB if core > 0 else allneg
    t4 = lambda *ms: np.concatenate(ms, 1)
    il4 = lambda m: np.repeat(m, 4, axis=1)
    mb = np.zeros((128, NMB, 512), np.float32)
    mb[:, 0] = t4(prevA, prevA, prevA, prevA)
    mb[:, 1] = t4(haloA, prevA, prevA, prevA)
    mb[:, 2] = t4(cur, cur, cur, cur)
    mb[:, 3] = il4(prevA)
    mb[:, 4] = il4(haloA)
    mb[:, 5] = il4(cur)
    for k in range(4):
        mb[:, 6 + k] = np.repeat(haloA[:, 32 * k:32 * k + 32], 16, axis=1)
        mb[:, 10 + k] = np.repeat(cur[:, 32 * k:32 * k + 32], 16, axis=1)
    mb[:, 14] = t4(prevB, prevB, prevB, prevB)
    mb[:, 15] = t4(haloB, prevB, prevB, prevB)
    return mb


def _consts():
    c = np.zeros((128, 6, 128), np.float32)
    c[:, 0] = np.eye(128)
    pa = np.zeros((128, 128), np.float32)
    for i in range(128):
        pa[i, (i + 64) % 128] = 1.0
    c[:, 1] = pa
    pb = np.zeros((128, 128), np.float32)
    for i in range(128):
        base = (i // 64) * 64
        pb[i, base + ((i - base) + 32) % 64] = 1.0
    c[:, 2] = pb
    c[:, 3] = 1.0
    c[:, 4, 0:64] = 1.0
    c[:, 5, 64:128] = 1.0
    return c


def prepare_inputs(x, w_in, b_in, sinks, g_mix_a, g_mix_b, w_out, ln1_g, ln1_b, w_up, w_down, ln2_g, ln2_b):
    f = np.float32
    x = np.asarray(x, f)[0]
    w_in = np.asarray(w_in, f)[0]
    b_in = np.asarray(b_in, f)[0]
    sinks = np.asarray(sinks, f)[0]
    w_out = np.asarray(w_out, f)[0]
    w_up = np.asarray(w_up, f)[0]
    w_down = np.asarray(w_down, f)[0]
    cols = []
    for ch in range(32):
        cols.append(np.arange(ch * 128, ch * 128 + 128))
    kb0 = 4096 + np.arange(64)
    kb1 = 4096 + 64 + np.arange(64)
    cols.append(np.concatenate([kb0, kb0]))
    cols.append(np.concatenate([kb1, kb1]))
    cols.append(4096 + 128 + np.arange(128))
    cols = np.stack(cols)
    wsel = w_in[:, cols]
    win = np.ascontiguousarray(wsel.reshape(KC, 128, NCH_IN, 128).transpose(2, 1, 0, 3))
    binr = np.ascontiguousarray(b_in[cols].T)
    wout = np.ascontiguousarray(w_out.reshape(KC, 128, KC, 128).transpose(2, 1, 0, 3))
    gate = w_up[:, :DFF].reshape(KC, 128, FC, 128)
    up = w_up[:, DFF:].reshape(KC, 128, FC, 128)
    wup = np.ascontiguousarray(np.concatenate([gate, up], -1).transpose(2, 1, 0, 3))
    wdn = np.ascontiguousarray(w_down.reshape(FC, 128, KC, 128).transpose(2, 1, 0, 3))
    gm = np.concatenate([np.asarray(g_mix_a, f)[0], np.asarray(g_mix_b, f)[0]])
    gmix = np.ascontiguousarray(gm.reshape(KC, 128).T)
    lnp = np.ascontiguousarray(np.stack([np.asarray(a, f)[0].reshape(KC, 128).T
                                         for a in (ln1_g, ln1_b, ln2_g, ln2_b)], 1))
    sk = np.zeros((128, 8), f)
    for j in range(8):
        sk[0:64, j] = sinks[2 * j]
        sk[64:128, j] = sinks[2 * j + 1]
    consts = _consts()
    shared = dict(win=win, bin=binr, wout=wout, wup=wup, wdn=wdn, gmix=gmix, lnp=lnp, sinks=sk, consts=consts)
    in_maps = []
    xpad = np.concatenate([np.zeros((TOK, D), f), x], 0)
    for c in range(NCORES):
        t0 = c * TOK
        xe = xpad[t0:t0 + 2 * TOK]
        xTe = xe.T.reshape(KC, 128, 2, TOK)
        xT = np.ascontiguousarray(xTe.transpose(2, 1, 0, 3))
        xres = np.ascontiguousarray(xe[TOK:].T.reshape(KC, 128, TOK))
        pos = np.arange(t0 - TOK, t0 + TOK)
        rA = _rope_tables(pos, 128, 1).reshape(128, 2, 2, TOK).transpose(2, 0, 1, 3)
        rB = _rope_tables(pos, 64, 2).reshape(128, 2, 2, TOK).transpose(2, 0, 1, 3)
        m = dict(shared)
        m.update(xT=xT, xres=xres, ropeA=np.ascontiguousarray(rA), ropeB=np.ascontiguousarray(rB), maskb=_masks(c))
        in_maps.append(m)
    return in_maps


_NC_CACHE = {}


def kernel(**inputs):
    in_maps = prepare_inputs(**inputs)
    if "nc" not in _NC_CACHE:
        _NC_CACHE["nc"] = build()
    nc = _NC_CACHE["nc"]
    res = run_bass_kernel_spmd(nc, in_maps, core_ids=list(range(NCORES)))
    outs = []
    for c in range(NCORES):
        oT = res.results[c]["outT"]
        outs.append(oT.reshape(D, TOK).T)
    out = np.concatenate(outs, 0)[None].astype(np.float32)
    return np.ascontiguousarray(out)
```
